# Optimizing a Trainium2 kernel written in Bass

```python
import jax, jax.numpy as jnp
from jax import lax
import numpy as np

D_MODEL = 1024
BATCH = 16
SEQ = 2048
DEPTH = 2
DEC_BATCH = 8
DEC_SEQ = 32
PAST_LEN = 4096

CHUNK = 64
D_MIX = D_MODEL
D_POOL = D_MIX // 4
POOL_WINDOWS = (2, 4, 8, 16)
POOL_GROUPS = len(POOL_WINDOWS)
POOL_GC = D_POOL // POOL_GROUPS
POOL_PAD = max(POOL_WINDOWS) - 1
D_CONV = D_MIX // 4
CONV_WIDTH = 31
CONV_PAD = CONV_WIDTH - 1
D_ATT = D_MIX // 2
HEAD_DIM = 64
N_HEADS = D_ATT // HEAD_DIM
D_IN = D_POOL + 2 * D_CONV + 3 * D_ATT + N_HEADS
D_FF = 2816
QBLK = 128
EPS = 1e-6

kernel_name = "hybrid_streaming_encoder_step"


def rms_norm(x, g):
    xf = x.astype(jnp.float32)
    y = xf * lax.rsqrt(jnp.mean(xf * xf, axis=-1, keepdims=True) + EPS)
    return (y * g.astype(jnp.float32)).astype(x.dtype)


def layer_norm(x, g, b):
    xf = x.astype(jnp.float32)
    mu = jnp.mean(xf, axis=-1, keepdims=True)
    var = jnp.mean(jnp.square(xf - mu), axis=-1, keepdims=True)
    y = (xf - mu) * lax.rsqrt(var + EPS) * g.astype(jnp.float32) + b.astype(jnp.float32)
    return y.astype(x.dtype)


def swiglu_ffn(x, g, w_gu, w_down):
    h = rms_norm(x, g) @ w_gu
    a, u = jnp.split(h, 2, axis=-1)
    return (jax.nn.silu(a) * u) @ w_down


def pool_mixer(u, prev, t0, w, scale):
    B, T, C = u.shape
    seq = jnp.concatenate([prev.astype(u.dtype), u], axis=1)
    c = jnp.cumsum(seq.astype(jnp.float32), axis=1)
    c = jnp.concatenate([jnp.zeros((B, 1, C), jnp.float32), c], axis=1)
    P = POOL_PAD
    pos = t0 + jnp.arange(T)
    outs = []
    for g, wsz in enumerate(POOL_WINDOWS):
        sl = slice(g * POOL_GC, (g + 1) * POOL_GC)
        s = c[:, P + 1:P + 1 + T, sl] - c[:, P + 1 - wsz:P + 1 - wsz + T, sl]
        cnt = jnp.minimum(pos + 1, wsz).astype(jnp.float32)
        outs.append(s / cnt[None, :, None])
    pooled = jnp.concatenate(outs, axis=-1)
    d = (pooled - u.astype(jnp.float32)).astype(u.dtype).reshape(B, T, POOL_GROUPS, POOL_GC)
    y = jnp.einsum('btgc,gcd->btgd', d, w).reshape(B, T, C) * scale
    return y, seq[:, -P:]


def conv_mixer(u_glu, prev, w, b, ln_g, ln_b):
    a, gate = jnp.split(u_glu, 2, axis=-1)
    z = a * jax.nn.sigmoid(gate)
    seq = jnp.concatenate([prev.astype(z.dtype), z], axis=1)
    y = lax.conv_general_dilated(
        seq, w[:, None, :].astype(z.dtype), window_strides=(1,), padding='VALID',
        dimension_numbers=('NWC', 'WIO', 'NWC'), feature_group_count=D_CONV) + b
    y = jax.nn.silu(layer_norm(y, ln_g, ln_b))
    return y, seq[:, -CONV_PAD:]


def fox_attention(q, k, v, cq, ck, q_start):
    B, T, H, Dh = q.shape
    S = k.shape[1]
    blk = QBLK if T % QBLK == 0 else T
    nb = T // blk
    qb = q.reshape(B, nb, blk, H, Dh).swapaxes(0, 1)
    cqb = cq.reshape(B, nb, blk, H).swapaxes(0, 1)
    starts = q_start + jnp.arange(nb) * blk
    kf = k.astype(jnp.float32)
    ckT = ck.transpose(0, 2, 1)[:, :, None, :]
    kpos = jnp.arange(S)
    scale = HEAD_DIM ** -0.5

    def block(args):
        qi, ci, st = args
        s = jnp.einsum('bqhd,bkhd->bhqk', qi.astype(jnp.float32), kf) * scale
        s = s + ci.transpose(0, 2, 1)[..., None] - ckT
        qpos = st + jnp.arange(blk)
        mask = kpos[None, :] <= qpos[:, None]
        s = jnp.where(mask[None, None], s, -jnp.inf)
        p = jax.nn.softmax(s, axis=-1)
        return jnp.einsum('bhqk,bkhd->bqhd', p.astype(v.dtype), v)

    o = lax.map(block, (qb, cqb, starts))
    return o.swapaxes(0, 1).reshape(B, T, H, Dh)


def mixer_sublayer(x, prev_pool, prev_conv, k_past, v_past, logf_past, t0,
                   mix_norm, w_in, w_out, pool_w, pool_scale, conv_w, conv_b,
                   conv_ln_g, conv_ln_b, q_norm, k_norm, forget_b):
    B, T, _ = x.shape
    u = rms_norm(x, mix_norm) @ w_in
    cuts = np.cumsum([D_POOL, 2 * D_CONV, D_ATT, D_ATT, D_ATT]).tolist()
    u_pool, u_glu, u_q, u_k, u_v, u_f = jnp.split(u, cuts, axis=-1)
    y_pool, new_pool = pool_mixer(u_pool, prev_pool, t0, pool_w, pool_scale)
    y_conv, new_conv = conv_mixer(u_glu, prev_conv, conv_w, conv_b, conv_ln_g, conv_ln_b)
    q = rms_norm(u_q.reshape(B, T, N_HEADS, HEAD_DIM), q_norm)
    k = rms_norm(u_k.reshape(B, T, N_HEADS, HEAD_DIM), k_norm)
    v = u_v.reshape(B, T, N_HEADS, HEAD_DIM)
    logf = jax.nn.log_sigmoid((u_f + forget_b).astype(jnp.float32))
    if k_past is None:
        k_all, v_all, logf_all, q_start = k, v, logf, 0
    else:
        k_all = jnp.concatenate([k_past.astype(k.dtype), k], axis=1)
        v_all = jnp.concatenate([v_past.astype(v.dtype), v], axis=1)
        logf_all = jnp.concatenate([logf_past.astype(jnp.float32), logf], axis=1)
        q_start = k_past.shape[1]
    c_all = jnp.cumsum(logf_all, axis=1)
    y_att = fox_attention(q, k_all, v_all, c_all[:, q_start:], c_all, q_start)
    y = jnp.concatenate([y_pool, y_conv, y_att.reshape(B, T, D_ATT)], axis=-1) @ w_out
    return x + y, new_pool, new_conv, k, v, logf


def setup_inputs(seed: int = 0) -> dict:
    key = jax.random.key(seed)
    ks = jax.random.split(key, 26)
    nrm = jax.random.normal
    f32 = jnp.float32
    return {
        "x_prompt": nrm(ks[0], (BATCH, SEQ, D_MODEL), f32),
        "x_sample": nrm(ks[1], (DEC_BATCH, DEC_SEQ, D_MODEL), f32),
        "state_pool": nrm(ks[2], (DEPTH, DEC_BATCH, POOL_PAD, D_POOL), f32),
        "state_conv": 0.5 * nrm(ks[3], (DEPTH, DEC_BATCH, CONV_PAD, D_CONV), f32),
        "cache_k": nrm(ks[4], (DEPTH, DEC_BATCH, PAST_LEN, N_HEADS, HEAD_DIM), f32),
        "cache_v": nrm(ks[5], (DEPTH, DEC_BATCH, PAST_LEN, N_HEADS, HEAD_DIM), f32),
        "cache_logf": jax.nn.log_sigmoid(2.5 + nrm(ks[6], (DEPTH, DEC_BATCH, PAST_LEN, N_HEADS), f32)),
        "ffn1_norm": 1.0 + 0.1 * nrm(ks[7], (DEPTH, D_MODEL), f32),
        "ffn1_w_gu": nrm(ks[8], (DEPTH, D_MODEL, 2 * D_FF), f32) * D_MODEL ** -0.5,
        "ffn1_w_down": nrm(ks[9], (DEPTH, D_FF, D_MODEL), f32) * D_FF ** -0.5,
        "mix_norm": 1.0 + 0.1 * nrm(ks[10], (DEPTH, D_MODEL), f32),
        "w_in": nrm(ks[11], (DEPTH, D_MODEL, D_IN), f32) * D_MODEL ** -0.5,
        "w_out": nrm(ks[12], (DEPTH, D_MIX, D_MODEL), f32) * D_MIX ** -0.5,
        "pool_w": nrm(ks[13], (DEPTH, POOL_GROUPS, POOL_GC, POOL_GC), f32) * POOL_GC ** -0.5,
        "pool_scale": 1.0 + 0.1 * nrm(ks[14], (DEPTH, D_POOL), f32),
        "conv_w": nrm(ks[15], (DEPTH, CONV_WIDTH, D_CONV), f32) * CONV_WIDTH ** -0.5,
        "conv_b": 0.02 * nrm(ks[16], (DEPTH, D_CONV), f32),
        "conv_ln_g": 1.0 + 0.1 * nrm(ks[17], (DEPTH, D_CONV), f32),
        "conv_ln_b": 0.02 * nrm(ks[18], (DEPTH, D_CONV), f32),
        "q_norm": 1.0 + 0.1 * nrm(ks[19], (DEPTH, HEAD_DIM), f32),
        "k_norm": 1.0 + 0.1 * nrm(ks[20], (DEPTH, HEAD_DIM), f32),
        "forget_b": jax.random.uniform(ks[21], (DEPTH, N_HEADS), f32, minval=1.0, maxval=4.0),
        "ffn2_norm": 1.0 + 0.1 * nrm(ks[22], (DEPTH, D_MODEL), f32),
        "ffn2_w_gu": nrm(ks[23], (DEPTH, D_MODEL, 2 * D_FF), f32) * D_MODEL ** -0.5,
        "ffn2_w_down": nrm(ks[24], (DEPTH, D_FF, D_MODEL), f32) * D_FF ** -0.5,
    }


def reference(x_prompt, x_sample, state_pool, state_conv, cache_k, cache_v, cache_logf,
              ffn1_norm, ffn1_w_gu, ffn1_w_down, mix_norm, w_in, w_out, pool_w, pool_scale,
              conv_w, conv_b, conv_ln_g, conv_ln_b, q_norm, k_norm, forget_b,
              ffn2_norm, ffn2_w_gu, ffn2_w_down):
    xp, xs = x_prompt, x_sample
    Bp = xp.shape[0]
    pool_p, pool_s, conv_p, conv_s = [], [], [], []
    kp, vp, fp, ksl, vsl, fsl = [], [], [], [], [], []
    for l in range(DEPTH):
        mix_w = dict(mix_norm=mix_norm[l], w_in=w_in[l], w_out=w_out[l], pool_w=pool_w[l],
                     pool_scale=pool_scale[l], conv_w=conv_w[l], conv_b=conv_b[l],
                     conv_ln_g=conv_ln_g[l], conv_ln_b=conv_ln_b[l], q_norm=q_norm[l],
                     k_norm=k_norm[l], forget_b=forget_b[l])
        xp = xp + 0.5 * swiglu_ffn(xp, ffn1_norm[l], ffn1_w_gu[l], ffn1_w_down[l])
        xp, npool, nconv, nk, nv, nf = mixer_sublayer(
            xp, jnp.zeros((Bp, POOL_PAD, D_POOL), xp.dtype), jnp.zeros((Bp, CONV_PAD, D_CONV), xp.dtype),
            None, None, None, 0, **mix_w)
        xp = xp + 0.5 * swiglu_ffn(xp, ffn2_norm[l], ffn2_w_gu[l], ffn2_w_down[l])
        pool_p.append(npool); conv_p.append(nconv); kp.append(nk); vp.append(nv); fp.append(nf)
        xs = xs + 0.5 * swiglu_ffn(xs, ffn1_norm[l], ffn1_w_gu[l], ffn1_w_down[l])
        xs, npool, nconv, nk, nv, nf = mixer_sublayer(
            xs, state_pool[l], state_conv[l], cache_k[l], cache_v[l], cache_logf[l],
            cache_k.shape[2], **mix_w)
        xs = xs + 0.5 * swiglu_ffn(xs, ffn2_norm[l], ffn2_w_gu[l], ffn2_w_down[l])
        pool_s.append(npool); conv_s.append(nconv); ksl.append(nk); vsl.append(nv); fsl.append(nf)
    return (xp, xs,
            jnp.stack(pool_p), jnp.stack(pool_s),
            jnp.stack(conv_p), jnp.stack(conv_s),
            jnp.stack(kp), jnp.stack(vp), jnp.stack(fp),
            jnp.stack(ksl), jnp.stack(vsl), jnp.stack(fsl))
```

```python
import numpy as np
import concourse.bass as bass
import concourse.mybir as mybir
from concourse.bass_utils import run_bass_kernel_spmd

F32 = mybir.dt.float32
BF16 = mybir.dt.bfloat16
AF = mybir.ActivationFunctionType
ALU = mybir.AluOpType
AX = mybir.AxisListType

_DT_SIZE = {F32: 4, BF16: 2}


class Op:
    __slots__ = ("eng", "fn", "deps", "is_dma", "signal", "sem", "val", "idx", "seq")

    def __init__(self, eng, fn, is_dma):
        self.eng = eng
        self.fn = fn
        self.deps = []
        self.is_dma = is_dma
        self.signal = False
        self.sem = None
        self.val = 0


class Prog:
    ENGS = ("pe", "act", "dve", "pool", "sp")

    def __init__(self, nc, n_dma_sems=14):
        self.nc = nc
        self.ops = {e: [] for e in self.ENGS}
        self.acc = {}
        self.n_dma_sems = n_dma_sems
        self.tracked = set()
        self.nbuf = 0
        self.waited = {e: {} for e in self.ENGS}
        self.psum = set()

    def sb(self, name, shape, dt):
        t = self.nc.alloc_sbuf_tensor(name, list(shape), dt)
        self.tracked.add(name)
        return t.ap()

    def ps(self, name, shape, dt=F32):
        t = self.nc.alloc_psum_tensor(name, list(shape), dt)
        self.tracked.add(name)
        self.psum.add(name)
        return t.ap()

    def _regions(self, ap):
        name = ap.tensor.name
        if name not in self.tracked:
            return ()
        pat = ap.ap
        off = ap.offset
        pstride = pat[0][0]
        if pstride <= 0:
            pstride = 1 << 40
        p0 = off // pstride
        f0 = off % pstride
        p1 = p0 + pat[0][1]
        sz = _DT_SIZE.get(ap.dtype, 4)
        free = list(pat[1:])
        outer = [(0, 1)]
        if len(free) >= 2 and free[0][1] <= 40 and free[0][0] > 0:
            outer = [(free[0][0], free[0][1])]
            free = free[1:]
        ext = 0
        for st, cnt in free:
            ext += abs(st) * (cnt - 1)
        res = []
        ost, ocnt = outer[0]
        for i in range(ocnt):
            a = f0 + i * ost
            res.append((name, p0, p1, a * sz, (a + ext + 1) * sz))
        return res

    def op(self, eng, fn, reads=(), writes=(), dma=False):
        o = Op(eng, fn, dma)
        deps = set()
        rregs = []
        wregs = []
        for ap in reads:
            rregs.extend(self._regions(ap))
        for ap in writes:
            wregs.extend(self._regions(ap))
        for (name, p0, p1, f0, f1) in rregs:
            lst = self.acc.get(name)
            if not lst:
                continue
            if name in self.psum:
                for (q0, q1, g0, g1, po, w) in lst:
                    if po.eng != eng or (w and q0 < p1 and p0 < q1 and g0 < f1 and f0 < g1):
                        deps.add(po)
                continue
            for (q0, q1, g0, g1, po, w) in lst:
                if w and q0 < p1 and p0 < q1 and g0 < f1 and f0 < g1:
                    deps.add(po)
        for (name, p0, p1, f0, f1) in wregs:
            lst = self.acc.get(name)
            if not lst:
                continue
            keep = []
            isps = name in self.psum
            for ent in lst:
                (q0, q1, g0, g1, po, w) = ent
                if isps and po.eng != eng:
                    deps.add(po)
                if q0 < p1 and p0 < q1 and g0 < f1 and f0 < g1:
                    deps.add(po)
                    if q0 >= p0 and q1 <= p1 and g0 >= f0 and g1 <= f1:
                        continue
                keep.append(ent)
            self.acc[name] = keep
        for (name, p0, p1, f0, f1) in rregs:
            self.acc.setdefault(name, []).append((p0, p1, f0, f1, o, False))
        for (name, p0, p1, f0, f1) in wregs:
            self.acc.setdefault(name, []).append((p0, p1, f0, f1, o, True))
        deps.discard(o)
        best = {}
        for d in deps:
            if d.is_dma:
                o.deps.append(d)
                continue
            if d.eng == "pe" and eng == "pe":
                continue
            b = best.get(d.eng)
            if b is None or d.seq > b.seq:
                best[d.eng] = d
        for d in best.values():
            w = self.waited[eng].get(d.eng, -1)
            if d.seq <= w:
                continue
            self.waited[eng][d.eng] = d.seq
            o.deps.append(d)
            d.signal = True
        o.seq = len(self.ops[eng])
        self.ops[eng].append(o)
        return o

    def mm(self, out, lhsT, rhs, start=True, stop=True, skip=False):
        if skip:
            return self.op("pe", lambda e: e.matmul(out, lhsT, rhs, start=start, stop=stop, skip_group_check=True),
                           reads=[lhsT, rhs], writes=[out])
        return self.op("pe", lambda e: e.matmul(out, lhsT, rhs, start=start, stop=stop),
                       reads=[lhsT, rhs], writes=[out])

    def act(self, out, in_, func, bias=None, scale=None, eng="act", extra_reads=()):
        kw = {}
        rd = [in_] + list(extra_reads)
        if bias is not None:
            kw["bias"] = bias
            if not isinstance(bias, (int, float)):
                rd.append(bias)
        if scale is not None:
            kw["scale"] = scale
            if not isinstance(scale, (int, float)):
                rd.append(scale)
        return self.op("act", lambda e: e.activation(out, in_, func, **kw), reads=rd, writes=[out])

    def tt(self, eng, out, in0, in1, op):
        return self.op(eng, lambda e: e.tensor_tensor(out, in0, in1, op), reads=[in0, in1], writes=[out])

    def ts(self, eng, out, in0, s1, s2, op0, op1=None):
        rd = [in0]
        if not isinstance(s1, (int, float)):
            rd.append(s1)
        if s2 is not None and not isinstance(s2, (int, float)):
            rd.append(s2)
        if op1 is None:
            return self.op(eng, lambda e: e.tensor_scalar(out, in0, s1, None, op0), reads=rd, writes=[out])
        return self.op(eng, lambda e: e.tensor_scalar(out, in0, s1, s2, op0, op1), reads=rd, writes=[out])

    def stt(self, eng, out, in0, scalar, in1, op0, op1):
        rd = [in0, in1]
        if not isinstance(scalar, (int, float)):
            rd.append(scalar)
        return self.op(eng, lambda e: e.scalar_tensor_tensor(out, in0, scalar, in1, op0, op1),
                       reads=rd, writes=[out])

    def copy(self, eng, out, in_):
        if eng == "act":
            return self.op("act", lambda e: e.copy(out, in_), reads=[in_], writes=[out])
        return self.op(eng, lambda e: e.tensor_copy(out, in_), reads=[in_], writes=[out])

    def memset(self, eng, out, val):
        return self.op(eng, lambda e: e.memset(out, val), writes=[out])

    def dma(self, q, out, in_):
        return self.op(q, lambda e: e.dma_start(out, in_), reads=[in_], writes=[out], dma=True)

    def emit(self):
        nc = self.nc
        esem = {e: nc.alloc_semaphore("S_" + e) for e in self.ENGS}
        dsem = {e: [nc.alloc_semaphore("D_%s_%d" % (e, i)) for i in range(self.n_dma_sems)]
                for e in ("pool", "sp", "act")}
        dcnt = {e: [0] * self.n_dma_sems for e in dsem}
        drr = {e: 0 for e in dsem}
        final_waits = []
        for e in self.ENGS:
            if self.ops[e]:
                self.ops[e][-1].signal = True
        for e in self.ENGS:
            cnt = 0
            for o in self.ops[e]:
                if o.is_dma:
                    k = drr[e]
                    drr[e] = (k + 1) % self.n_dma_sems
                    dcnt[e][k] += 16
                    o.sem = dsem[e][k]
                    o.val = dcnt[e][k]
                    o.signal = True
                elif o.signal:
                    cnt += 1
                    o.sem = esem[e]
                    o.val = cnt
        for e in dsem:
            for k in range(self.n_dma_sems):
                if dcnt[e][k]:
                    final_waits.append((dsem[e][k], dcnt[e][k]))
        ops = self.ops

        def run(engname, eobj, last=False):
            waited = {}
            for o in ops[engname]:
                need = {}
                for d in o.deps:
                    key = d.sem.num
                    if need.get(key, (None, 0))[1] < d.val:
                        need[key] = (d.sem, d.val)
                if o.is_dma and o.val > 16:
                    key = o.sem.num
                    if need.get(key, (None, 0))[1] < o.val - 16:
                        need[key] = (o.sem, o.val - 16)
                for key, (s, v) in need.items():
                    if waited.get(key, 0) >= v:
                        continue
                    eobj.wait_ge(s, v)
                    waited[key] = v
                ins = o.fn(eobj)
                if o.signal:
                    ins.then_inc(o.sem, 16 if o.is_dma else 1)
            if last:
                for s, v in final_waits:
                    if waited.get(s.num, 0) < v:
                        eobj.wait_ge(s, v)
                for e2 in self.ENGS:
                    if e2 == engname:
                        continue
                    sig = [o for o in ops[e2] if o.signal and not o.is_dma]
                    if sig:
                        eobj.wait_ge(esem[e2], sig[-1].val)

        with nc.Block() as block:
            @block.tensor
            def _(e):
                run("pe", e)

            @block.scalar
            def _(e):
                run("act", e)

            @block.vector
            def _(e):
                run("dve", e)

            @block.gpsimd
            def _(e):
                run("pool", e)

            @block.sync
            def _(e):
                run("sp", e, last=True)
D = 1024
NCH = 8
SEQ = 2048
DEC = 32
PAST = 4096
DFF = 2816
NFC = 22
DIN = 2312
EPS = 1e-6
TTMAX = SEQ + DEC
ARENA_BYTES = 66 * 1024
FGROUPS = [(0, 6), (6, 6), (12, 5), (17, 5)]
import os as _os
ATT_LEVEL = int(_os.environ.get('ATT_LEVEL', '4'))


class Arena:
    def __init__(self, ap_bf16, nbytes):
        self.ap = ap_bf16
        self.nbytes = nbytes
        self.off = 0

    def reset(self):
        self.off = 0

    def alloc(self, shape, dt):
        sz = _DT_SIZE[dt]
        n = 1
        for s in shape[1:]:
            n *= s
        nb = (n * sz + 63) // 64 * 64
        assert self.off + nb <= self.nbytes, ("arena overflow", self.off, nb, shape)
        v = self.ap[:, self.off // 2:(self.off + n * sz) // 2]
        self.off += nb
        if dt != BF16:
            v = v.bitcast(dt)
        if len(shape) == 3:
            v = v.rearrange("p (a b) -> p a b", a=shape[1])
        elif len(shape) == 4:
            v = v.rearrange("p (a b c) -> p a b c", a=shape[1], b=shape[2])
        if shape[0] < 128:
            v = v[0:shape[0]]
        return v


class Seq:
    def __init__(self, kind, col0, T, b):
        self.kind = kind
        self.col0 = col0
        self.T = T
        self.b = b
        self.ttiles = []
        t = 0
        while t < T:
            n = min(512, T - t)
            self.ttiles.append((col0 + t, n))
            t += n
        self.ktiles = []
        t = 0
        while t < T:
            n = min(128, T - t)
            self.ktiles.append((t, n))
            t += n


def build_program(stage=None, npass=2):
    nc = bass.Bass("TRN2", target_bir_lowering=False)
    P = Prog(nc)

    def din(name, shape):
        return nc.dram_tensor(name, list(shape), F32, kind="ExternalInput").ap()

    def dout(name, shape):
        return nc.dram_tensor(name, list(shape), F32, kind="ExternalOutput").ap()

    xp = din("xp", [2, SEQ, D])
    xs = din("xs", [DEC, D])
    spool = din("spool", [2, 15, 256])
    sconv = din("sconv", [2, 30, 256])
    ck = din("ck", [2, PAST, 512])
    cv = din("cv", [2, PAST, 512])
    clf = din("clf", [2, PAST, 8])
    W = {}
    for nm, shp in [("ffn1_norm", [2, D]), ("ffn1_w_gu", [2, D, 2 * DFF]), ("ffn1_w_down", [2, DFF, D]),
                    ("mix_norm", [2, D]), ("w_in", [2, D, DIN]), ("w_out", [2, D, D]),
                    ("pool_w", [2, 4, 64, 64]), ("pool_scale", [2, 256]), ("conv_w", [2, 31, 256]),
                    ("conv_b", [2, 256]), ("conv_ln_g", [2, 256]), ("conv_ln_b", [2, 256]),
                    ("q_norm", [2, 64]), ("k_norm", [2, 64]), ("forget_b", [2, 8]),
                    ("ffn2_norm", [2, D]), ("ffn2_w_gu", [2, D, 2 * DFF]), ("ffn2_w_down", [2, DFF, D])]:
        W[nm] = din(nm, shp)
    c_ident = din("c_ident", [128, 128])
    c_tri = din("c_tri", [128, 128])
    c_mask = din("c_mask", [128, 128])
    c_invcnt = din("c_invcnt", [128, 2, 16])

    y_p = dout("y_p", [2, SEQ, D])
    y_s = dout("y_s", [DEC, D])
    npool_p = dout("npool_p", [2, 2, 15, 256])
    npool_s = dout("npool_s", [2, 15, 256])
    nconv_p = dout("nconv_p", [2, 2, 30, 256])
    nconv_s = dout("nconv_s", [2, 30, 256])
    nk_p = dout("nk_p", [2, 2, SEQ, 512])
    nv_p = dout("nv_p", [2, 2, SEQ, 512])
    nf_p = dout("nf_p", [2, 2, SEQ, 8])
    nk_s = dout("nk_s", [2, DEC, 512])
    nv_s = dout("nv_s", [2, DEC, 512])
    nf_s = dout("nf_s", [2, DEC, 8])

    x = P.sb("x", [128, NCH, TTMAX], F32)
    xn = P.sb("xn", [128, NCH, TTMAX], BF16)
    cat = P.sb("cat", [128, NCH, TTMAX], BF16)
    arena_ap = P.sb("arena", [128, ARENA_BYTES // 2], BF16)
    A = Arena(arena_ap, ARENA_BYTES)
    ident32 = P.sb("ident32", [128, 128], F32)
    identb = P.sb("identb", [128, 128], BF16)
    tri32 = P.sb("tri32", [128, 128], F32)
    ones32 = P.sb("ones32", [128, 128], F32)
    onesb = P.sb("onesb", [128, 128], BF16)
    maskb = P.sb("maskb", [128, 128], BF16)
    invcnt = P.sb("invcnt", [128, 2, 16], F32)
    epsT = P.sb("epsT", [128, 1], F32)
    prm = P.sb("prm", [128, 64], F32)
    cwT = P.sb("cwT", [128, 124], F32)
    gqk = P.sb("gqk", [128, 2, 4, 64], F32)
    fbT = P.sb("fbT", [128, 2, 8], F32)
    pwb = P.sb("pwb", [128, 2, 2, 128], BF16)
    pb = [P.ps("pb%d" % i, [128, 512], F32) for i in range(8)]
    wf = P.sb("wf", [128, NCH, 8], BF16)
    lf = P.sb("lf", [128, 33, 8], F32)
    ua = P.sb("ua", [128, 33, 8], F32)
    ctok = P.sb("ctok", [128, 8, 33], F32)
    rall = P.sb("rall", [128, 33, 8], F32)
    Sp = [P.sb("Sp%d" % i, [128, 8], F32) for i in range(2)]
    alpha = P.sb("alpha", [128, 16, 8], F32)

    for t_ in (lf, ua, ctok, rall, alpha, Sp[0], Sp[1]):
        P.memset("dve", t_, 0.0)
    P.dma("sp", ident32, c_ident)
    P.dma("sp", tri32, c_tri)
    P.dma("pool", identb, c_ident)
    P.dma("pool", maskb, c_mask)
    P.dma("sp", invcnt, c_invcnt)
    P.memset("dve", ones32, 1.0)
    P.memset("dve", onesb, 1.0)
    P.memset("dve", epsT, EPS)
    A.reset()
    rows = A.alloc([64, 128], F32)
    rows2 = A.alloc([124, 128], F32)
    r = 0
    PRM = {}
    for nm, nr in [("ffn1_norm", 16), ("mix_norm", 16), ("ffn2_norm", 16), ("pool_scale", 4),
                   ("conv_b", 4), ("conv_ln_g", 4), ("conv_ln_b", 4)]:
        P.dma("sp", rows[r:r + nr, :], W[nm].rearrange("l (c p) -> (l c) p", p=128))
        PRM[nm] = r
        r += nr
    P.dma("sp", rows2, W["conv_w"].rearrange("l j (c p) -> (l j c) p", p=128))
    P.mm(pb[7][:, 0:64], rows, ident32[0:64, 0:64])
    P.copy("dve", prm, pb[7][:, 0:64])
    P.mm(pb[7][:, 128:252], rows2, ident32[0:124, 0:124])
    P.copy("dve", cwT, pb[7][:, 128:252])
    for l in range(2):
        for k in range(4):
            src = W["q_norm"] if k < 2 else W["k_norm"]
            P.dma("sp", gqk[:, l, k, :], src[l:l + 1, :].partition_broadcast(128))
        P.dma("sp", fbT[:, l, :], W["forget_b"][l:l + 1, :].partition_broadcast(128))
    P.memset("dve", pwb, 0.0)
    for l in range(2):
        for cc in range(2):
            for g2 in range(2):
                P.dma("pool", pwb[g2 * 64:(g2 + 1) * 64, l, cc, g2 * 64:(g2 + 1) * 64],
                      W["pool_w"][l, 2 * cc + g2])

    def gvec(nm, l, c):
        base = PRM[nm]
        nper = 8 if nm.endswith("norm") else 2
        k = base + l * nper + c
        return prm[:, k:k + 1]

    def load_x(seq):
        src = xp[seq.b] if seq.kind == "prompt" else xs
        for (t0, n) in seq.ktiles:
            xt = A.alloc([128, D], F32)
            P.dma("sp", xt[0:n, :], src[t0:t0 + n, :])
            for half in range(2):
                ps = pb[6 + half]
                for q in range(4):
                    dc = half * 4 + q
                    P.mm(ps[:, q * 128:q * 128 + n], xt[0:n, dc * 128:(dc + 1) * 128], ident32[0:n, 0:n])
                dst = x[:, half * 4:half * 4 + 4, seq.col0 + t0:seq.col0 + t0 + n]
                srcp = ps.rearrange("p (a b) -> p a b", a=4)[:, :, 0:n]
                P.copy("act" if half == 0 else "dve", dst, srcp)
            if A.off > ARENA_BYTES - 8192:
                A.reset()

    def store_y(seq):
        dst = y_p[seq.b] if seq.kind == "prompt" else y_s
        for (t0, n) in seq.ktiles:
            yt = A.alloc([128, D], F32)
            for half in range(2):
                ps = pb[6 + half]
                for q in range(4):
                    dc = half * 4 + q
                    P.mm(ps[0:n, q * 128:(q + 1) * 128], x[:, dc, seq.col0 + t0:seq.col0 + t0 + n], ident32)
                P.copy("act" if half == 0 else "dve", yt[0:n, half * 512:(half + 1) * 512], ps[0:n, :])
            P.dma("sp", dst[t0:t0 + n, :], yt[0:n, :])
            if A.off > ARENA_BYTES - 8192:
                A.reset()

    def norm_bufs(arena):
        sqs = [arena.alloc([128, NCH, 512], BF16) for _ in range(2)]
        sds = [arena.alloc([128, 512], F32) for _ in range(2)]
        rss = [arena.alloc([128, 512], F32) for _ in range(2)]
        return {"sq": sqs, "sd": sds, "rs": rss, "k": 0}

    def norm_tile(nm, l, c0, n, nb):
        k_ = nb["k"]
        nb["k"] += 1
        sq, sd, rs = nb["sq"][k_ % 2], nb["sd"][k_ % 2], nb["rs"][k_ % 2]
        P.act(sq[:, :, 0:n], x[:, :, c0:c0 + n], AF.Square)
        for c in range(NCH):
            P.mm(pb[6][:, 0:n], onesb, sq[:, c, 0:n], start=(c == 0), stop=(c == NCH - 1))
        P.act(sd[:, 0:n], pb[6][:, 0:n], AF.Sqrt, bias=epsT, scale=1.0 / D)
        P.op("dve", lambda e, o=rs[:, 0:n], i=sd[:, 0:n]: e.reciprocal(o, i),
             reads=[sd[:, 0:n]], writes=[rs[:, 0:n]])
        for c in range(NCH):
            P.stt("dve", xn[:, c, c0:c0 + n], x[:, c, c0:c0 + n], gvec(nm, l, c), rs[:, 0:n],
                  ALU.mult, ALU.mult)

    def rmsnorm(seqs, nm, l):
        A.reset()
        nb = norm_bufs(A)
        for seq in seqs:
            for (c0, n) in seq.ttiles:
                norm_tile(nm, l, c0, n, nb)

    def ffn(seqs, l, which, prenormed=False, next_norm=None):
        if not prenormed:
            rmsnorm(seqs, which + "_norm", l)
        wgu_d = W[which + "_w_gu"][l].rearrange("(c p) f -> p c f", p=128)
        wd_d = W[which + "_w_down"][l].rearrange("(f p) d -> p f d", p=128)
        A.reset()
        g = A.alloc([128, 6, TTMAX], BF16)
        off_wgu = A.off
        wgu = [A.alloc([128, NCH, 256], BF16) for _ in range(3)]
        wd = [A.alloc([128, 6, D], BF16) for _ in range(2)]
        st = [A.alloc([128, 512], F32) for _ in range(2)]
        tiles = [tt for s in seqs for tt in s.ttiles]

        def load_gu(fc):
            b = wgu[fc % 3]
            P.dma("pool", b[:, :, 0:128], wgu_d[:, :, fc * 128:(fc + 1) * 128])
            P.dma("pool", b[:, :, 128:256], wgu_d[:, :, DFF + fc * 128:DFF + (fc + 1) * 128])

        def load_d(gi):
            f0, nf = FGROUPS[gi]
            P.dma("pool", wd[gi % 2][:, 0:nf, :], wd_d[:, f0:f0 + nf, :])

        for fc in range(3):
            load_gu(fc)
        load_d(0)
        load_d(1)
        it = 0
        cntd = {"ky": 0}
        for gi, (f0, nf) in enumerate(FGROUPS):
            for fl in range(nf):
                fc = f0 + fl
                b = wgu[fc % 3]
                for (c0, n) in tiles:
                    pa, pu, s = pb[it % 2], pb[2 + it % 2], st[it % 2]
                    it += 1
                    for c in range(NCH):
                        P.mm(pa[:, 0:n], b[:, c, 0:128], xn[:, c, c0:c0 + n], start=(c == 0), stop=(c == NCH - 1))
                    for c in range(NCH):
                        P.mm(pu[:, 0:n], b[:, c, 128:256], xn[:, c, c0:c0 + n], start=(c == 0), stop=(c == NCH - 1))
                    P.act(s[:, 0:n], pa[:, 0:n], AF.Silu)
                    P.tt("dve", g[:, fl, c0:c0 + n], pu[:, 0:n], s[:, 0:n], ALU.mult)
                if fc + 3 < NFC:
                    load_gu(fc + 3)
            wb = wd[gi % 2]
            last = (gi == len(FGROUPS) - 1)

            def down(dc, c0, n):
                py = pb[4 + cntd["ky"] % 2]
                cntd["ky"] += 1
                for fl in range(nf):
                    P.mm(py[:, 0:n], wb[:, fl, dc * 128:(dc + 1) * 128], g[:, fl, c0:c0 + n],
                         start=(fl == 0), stop=(fl == nf - 1))
                P.stt("dve", x[:, dc, c0:c0 + n], py[:, 0:n], 0.5, x[:, dc, c0:c0 + n], ALU.mult, ALU.add)

            if last and next_norm is not None:
                A2 = Arena(arena_ap, ARENA_BYTES)
                A2.off = off_wgu
                nb = norm_bufs(A2)
                prevt = None
                for (c0, n) in tiles:
                    for dc in range(NCH):
                        down(dc, c0, n)
                    if prevt is not None:
                        norm_tile(next_norm[0], next_norm[1], prevt[0], prevt[1], nb)
                    prevt = (c0, n)
                norm_tile(next_norm[0], next_norm[1], prevt[0], prevt[1], nb)
            else:
                for dc in range(NCH):
                    for (c0, n) in tiles:
                        down(dc, c0, n)
            if gi + 2 < len(FGROUPS):
                load_d(gi + 2)

    def transpose_out(src_cols, ncols, stg, col_off, ps):
        P.mm(ps[0:ncols, 0:128], src_cols, ident32)
        P.copy("act", stg[0:ncols, col_off:col_off + 128], ps[0:ncols, 0:128])

    def pool_part(seq, l):
        T = seq.T
        win_d = W["w_in"][l].rearrange("(c p) f -> p c f", p=128)
        A.reset()
        wp = [A.alloc([128, NCH, 128], BF16) for _ in range(2)]
        up = A.alloc([128, 16 + SEQ], F32)
        sa = A.alloc([128, 16 + SEQ], F32)
        sbf = A.alloc([128, 16 + SEQ], F32)
        dd = A.alloc([128, SEQ], BF16)
        stg = A.alloc([16, 256], F32)
        tmp16 = A.alloc([128, 16], F32)
        for cc in range(2):
            P.dma("pool", wp[cc], win_d[:, :, cc * 128:(cc + 1) * 128])
        if seq.kind == "sample":
            prev = A.alloc([16, 256], F32)
            P.dma("sp", prev[0:15, :], spool[l])
        kk = 0
        for cc in range(2):
            if seq.kind == "prompt":
                P.memset("dve", up[:, 0:16], 0.0)
            else:
                P.memset("dve", up[:, 0:1], 0.0)
                P.mm(pb[7][:, 0:15], prev[0:15, cc * 128:(cc + 1) * 128], ident32[0:15, 0:15])
                P.copy("act", up[:, 1:16], pb[7][:, 0:15])
            for (c0, n) in seq.ttiles:
                ps = pb[kk % 2]
                kk += 1
                for c in range(NCH):
                    P.mm(ps[:, 0:n], wp[cc][:, c, :], xn[:, c, c0:c0 + n], start=(c == 0), stop=(c == NCH - 1))
                t0 = c0 - seq.col0
                P.copy("act", up[:, 16 + t0:16 + t0 + n], ps[:, 0:n])
            L = 16 + T
            P.tt("dve", sa[:, 2:L], up[:, 2:L], up[:, 1:L - 1], ALU.add)
            P.tt("dve", sbf[:, 4:L], sa[:, 4:L], sa[:, 2:L - 2], ALU.add)
            if cc == 0:
                lo, hi, wl, wh = sa, sbf, 2.0, 4.0
            else:
                P.tt("dve", sa[:, 8:L], sbf[:, 8:L], sbf[:, 4:L - 4], ALU.add)
                P.tt("dve", sbf[:, 16:L], sa[:, 16:L], sa[:, 8:L - 8], ALU.add)
                lo, hi, wl, wh = sa, sbf, 8.0, 16.0
            for (pr, srcb, w) in ((slice(0, 64), lo, wl), (slice(64, 128), hi, wh)):
                P.stt("dve", dd[pr, 0:T], srcb[pr, 16:16 + T], 1.0 / w, up[pr, 16:16 + T], ALU.mult, ALU.subtract)
                if seq.kind == "prompt":
                    P.tt("dve", tmp16[pr, :], srcb[pr, 16:32], invcnt[pr, cc, :], ALU.mult)
                    P.tt("dve", dd[pr, 0:16], tmp16[pr, :], up[pr, 16:32], ALU.subtract)
            for (c0, n) in seq.ttiles:
                ps = pb[2 + kk % 2]
                kk += 1
                t0 = c0 - seq.col0
                P.mm(ps[:, 0:n], pwb[:, l, cc, :], dd[:, t0:t0 + n])
                P.act(cat[:, cc, c0:c0 + n], ps[:, 0:n], AF.Copy, scale=gvec("pool_scale", l, cc))
            transpose_out(up[:, T + 1:T + 16], 15, stg, cc * 128, pb[7][:, 256:384])
        dst = npool_p[l, seq.b] if seq.kind == "prompt" else npool_s[l]
        P.dma("sp", dst, stg[0:15, :])

    def conv_part(seq, l, hook=None):
        T = seq.T
        win_d = W["w_in"][l].rearrange("(c p) f -> p c f", p=128)
        A.reset()
        wa = [A.alloc([128, NCH, 128], BF16) for _ in range(2)]
        wg = [A.alloc([128, NCH, 128], BF16) for _ in range(2)]
        z = A.alloc([128, 2, 32 + SEQ], F32)
        acc = A.alloc([128, 2, SEQ], F32)
        sg = [A.alloc([128, 512], F32) for _ in range(2)]
        sq = A.alloc([128, 2, 512], F32)
        mu = A.alloc([128, 512], F32)
        m2 = A.alloc([128, 512], F32)
        sd = A.alloc([128, 512], F32)
        rs = A.alloc([128, 512], F32)
        t1 = [A.alloc([128, 512], F32) for _ in range(2)]
        stg = A.alloc([32, 256], F32)
        for cc in range(2):
            P.dma("pool", wa[cc], win_d[:, :, 256 + cc * 128:256 + (cc + 1) * 128])
            P.dma("pool", wg[cc], win_d[:, :, 512 + cc * 128:512 + (cc + 1) * 128])
        if seq.kind == "sample":
            prev = A.alloc([32, 256], F32)
            P.dma("sp", prev[0:30, :], sconv[l])
        kk = 0
        for cc in range(2):
            if seq.kind == "prompt":
                P.memset("dve", z[:, cc, 0:32], 0.0)
            else:
                P.memset("dve", z[:, cc, 0:2], 0.0)
                P.mm(pb[7][:, 0:30], prev[0:30, cc * 128:(cc + 1) * 128], ident32[0:30, 0:30])
                P.copy("act", z[:, cc, 2:32], pb[7][:, 0:30])
            for (c0, n) in seq.ttiles:
                pa, pg, s = pb[kk % 2], pb[2 + kk % 2], sg[kk % 2]
                kk += 1
                for c in range(NCH):
                    P.mm(pa[:, 0:n], wa[cc][:, c, :], xn[:, c, c0:c0 + n], start=(c == 0), stop=(c == NCH - 1))
                for c in range(NCH):
                    P.mm(pg[:, 0:n], wg[cc][:, c, :], xn[:, c, c0:c0 + n], start=(c == 0), stop=(c == NCH - 1))
                t0 = c0 - seq.col0
                P.act(s[:, 0:n], pg[:, 0:n], AF.Sigmoid)
                P.tt("dve", z[:, cc, 32 + t0:32 + t0 + n], pa[:, 0:n], s[:, 0:n], ALU.mult)
            transpose_out(z[:, cc, T + 2:T + 32], 30, stg, cc * 128, pb[7][:, 256:384])
        def tap(j, cc):
            k = (l * 31 + j) * 2 + cc
            return cwT[:, k:k + 1]
        for cc in range(2):
            P.ts("dve", acc[:, cc, 0:T], z[:, cc, 2:2 + T], tap(0, cc), gvec("conv_b", l, cc), ALU.mult, ALU.add)
        for j in range(1, 31):
            for cc in range(2):
                P.stt("dve", acc[:, cc, 0:T], z[:, cc, 2 + j:2 + j + T], tap(j, cc), acc[:, cc, 0:T],
                      ALU.mult, ALU.add)
                if hook is not None:
                    hook()
        dst = nconv_p[l, seq.b] if seq.kind == "prompt" else nconv_s[l]
        P.dma("sp", dst, stg[0:30, :])
        kk = 0
        for (c0, n) in seq.ttiles:
            t0 = c0 - seq.col0
            a2 = acc[:, :, t0:t0 + n]
            P.act(sq[:, :, 0:n], a2, AF.Square)
            for cc in range(2):
                P.mm(pb[4][:, 0:n], ones32, acc[:, cc, t0:t0 + n], start=(cc == 0), stop=(cc == 1))
            for cc in range(2):
                P.mm(pb[5][:, 0:n], ones32, sq[:, cc, 0:n], start=(cc == 0), stop=(cc == 1))
            P.act(mu[:, 0:n], pb[4][:, 0:n], AF.Copy, scale=1.0 / 256)
            P.tt("dve", m2[:, 0:n], mu[:, 0:n], mu[:, 0:n], ALU.mult)
            P.stt("dve", m2[:, 0:n], pb[5][:, 0:n], 1.0 / 256, m2[:, 0:n], ALU.mult, ALU.subtract)
            P.ts("dve", m2[:, 0:n], m2[:, 0:n], 0.0, None, ALU.max)
            P.act(sd[:, 0:n], m2[:, 0:n], AF.Sqrt, bias=epsT, scale=1.0)
            P.op("dve", lambda e, o=rs[:, 0:n], i=sd[:, 0:n]: e.reciprocal(o, i),
                 reads=[sd[:, 0:n]], writes=[rs[:, 0:n]])
            for cc in range(2):
                t = t1[kk % 2]
                kk += 1
                P.tt("dve", t[:, 0:n], acc[:, cc, t0:t0 + n], mu[:, 0:n], ALU.subtract)
                P.tt("dve", t[:, 0:n], t[:, 0:n], rs[:, 0:n], ALU.mult)
                P.act(cat[:, 2 + cc, c0:c0 + n], t[:, 0:n], AF.Silu,
                      bias=gvec("conv_ln_b", l, cc), scale=gvec("conv_ln_g", l, cc))

    def att_prologue(seq, l):
        T = seq.T
        sample = seq.kind == "sample"
        win_d = W["w_in"][l].rearrange("(c p) f -> p c f", p=128)
        NN = len(seq.ktiles)
        NP = PAST // 128 if sample else 0
        NT = NP + NN
        P.dma("pool", wf, win_d[:, :, 2304:2312])
        if sample:
            cl = clf[l].rearrange("(j p) h -> p j h", p=128)
            for q in range(4):
                P.dma("sp", lf[:, q * 8:(q + 1) * 8, :], cl[:, q * 8:(q + 1) * 8, :])
        for ti, (t0, n) in enumerate(seq.ktiles):
            ps = pb[6][0:n, ti * 8:(ti + 1) * 8]
            cs = seq.col0 + t0
            for c in range(NCH):
                P.mm(ps, xn[:, c, cs:cs + n], wf[:, c, :], start=(c == 0), stop=(c == NCH - 1))
            P.tt("dve", ua[0:n, ti, :], ps, fbT[0:n, l, :], ALU.add)
            yield
        nrow = seq.ktiles[0][1]
        uv = ua[0:nrow, 0:NN, :]
        P.act(uv, uv, AF.Exp, scale=-1.0)
        P.act(uv, uv, AF.Ln, bias=1.0)
        P.ts("dve", lf[0:nrow, NP:NP + NN, :], uv, -1.0, None, ALU.mult)
        if sample:
            P.dma("sp", nf_s[l], lf[0:nrow, NP, :])
        else:
            P.dma("sp", nf_p[l, seq.b].rearrange("(j p) h -> p j h", p=128), lf[:, 0:NN, :])
        yield
        P.memset("dve", Sp[0], 0.0)
        P.memset("dve", rall[:, 0, :], 0.0)
        for j in range(NT):
            n = 128 if j < NP else seq.ktiles[j - NP][1]
            ps = pb[6][0:n, 256 + (j % 16) * 8:256 + (j % 16) * 8 + 8]
            s_cur, s_nxt = Sp[j % 2], Sp[(j + 1) % 2]
            P.mm(ps, tri32[0:n, 0:n], lf[0:n, j, :], start=True, stop=(j == 0))
            if j > 0:
                P.mm(ps, ones32[:, 0:n], s_cur, start=False, stop=True)
                pr = pb[6][:, 384 + (j % 16) * 8:384 + (j % 16) * 8 + 8]
                P.mm(pr, ones32, s_cur)
                P.copy("act", rall[:, j, :], pr)
            P.copy("act", ctok[0:n, :, j], ps)
            if j + 1 < NT:
                P.tt("dve", s_nxt, s_cur, lf[:, j, :], ALU.add)
            yield
        if not sample:
            for I in range(4):
                P.tt("dve", alpha[:, 4 * I:4 * I + 4, :], rall[:, 4 * I:4 * I + 4, :],
                     rall[:, 4 * I:4 * I + 1, :].broadcast_to([128, 4, 8]), ALU.subtract)
            P.act(alpha, alpha, AF.Exp)
        yield

    def att_part(seq, l):
        T = seq.T
        sample = seq.kind == "sample"
        win_d = W["w_in"][l].rearrange("(c p) f -> p c f", p=128)
        NN = len(seq.ktiles)
        NP = PAST // 128 if sample else 0
        NT = NP + NN
        A.reset()
        biasb = [A.alloc([128, 2, 33], F32) for _ in range(2)]
        wqkv = [A.alloc([128, NCH, 384], BF16) for _ in range(2)]
        QT = A.alloc([128, SEQ], BF16)
        KT = A.alloc([128, SEQ], BF16)
        VA = A.alloc([128, 16, 2, 128], BF16)
        sqb = [A.alloc([128, 256], F32) for _ in range(2)]
        ssb = [A.alloc([128, 4], F32) for _ in range(2)]
        sdb = [A.alloc([128, 4], F32) for _ in range(2)]
        qkn = [A.alloc([128, 4, 64], F32) for _ in range(3)]
        vst = [A.alloc([128, 128], F32) for _ in range(3)]
        PTW = 128 if sample else 512
        PT = [A.alloc([128, 2, PTW], BF16) for _ in range(3)]
        rec = [A.alloc([128, 2, 128], F32) for _ in range(2)]
        if not sample:
            toff = [A.alloc([128, 2, 128], F32) for _ in range(2)]
        if sample:
            kraw = [A.alloc([128, 4, 128], F32) for _ in range(3)]
            vraw32 = [A.alloc([128, 4, 128], F32) for _ in range(3)]
            KTg = [A.alloc([128, 4, 128], BF16) for _ in range(2)]
            ebb = [A.alloc([128, 2, 33], F32) for _ in range(2)]
            vraw = [A.alloc([128, 4, 2, 128], BF16) for _ in range(3)]
            for v in vraw:
                P.memset("dve", v[:, :, :, 64:128], 1.0)
        P.memset("dve", VA[:, :, :, 64:128], 1.0)

        def load_qkv(p):
            b = wqkv[p % 2]
            for k in range(3):
                P.dma("pool", b[:, :, k * 128:(k + 1) * 128],
                      win_d[:, :, 768 + 512 * k + p * 128:768 + 512 * k + (p + 1) * 128])

        load_qkv(0)
        load_qkv(1)
        if ATT_LEVEL < 3:
            return
        qtiles = [(NP + i, t0, n) for i, (t0, n) in enumerate(seq.ktiles)]
        cnt = {"s": 0, "o": 0, "pt": 0, "kr": 0, "b": 0, "q": 0, "r": 0, "tp": 0}
        dstk = nk_s[l] if sample else nk_p[l, seq.b]
        dstv = nv_s[l] if sample else nv_p[l, seq.b]
        QKB = [pb[0], pb[2], pb[3]]
        TRB = [pb[1], pb[4]]
        for p in range(4):
            wb = wqkv[p % 2]
            def projA(ti, t0, n):
                cs = seq.col0 + t0
                q_ = cnt["q"]
                cnt["q"] += 1
                ps = QKB[q_ % 3]
                sq_, ss_, sd_ = sqb[q_ % 2], ssb[q_ % 2], sdb[q_ % 2]
                qk, vs = qkn[q_ % 3], vst[q_ % 3]
                for c in range(NCH):
                    P.mm(ps[0:n, 0:384], xn[:, c, cs:cs + n], wb[:, c, :], start=(c == 0), stop=(c == NCH - 1))
                P.act(sq_[0:n, :], ps[0:n, 0:256], AF.Square)
                P.op("dve", lambda e, o=ss_[0:n, :], i=sq_[0:n, :].rearrange("p (a b) -> p a b", a=4):
                     e.tensor_reduce(o, i, AX.X, ALU.add), reads=[sq_[0:n, :]], writes=[ss_[0:n, :]])
                P.act(sd_[0:n, :], ss_[0:n, :], AF.Sqrt, bias=epsT[0:n, :], scale=1.0 / 64)
                P.op("dve", lambda e, o=ss_[0:n, :], i=sd_[0:n, :]: e.reciprocal(o, i),
                     reads=[sd_[0:n, :]], writes=[ss_[0:n, :]])
                for k4 in range(4):
                    P.stt("dve", qk[0:n, k4, :], ps[0:n, k4 * 64:(k4 + 1) * 64], ss_[0:n, k4:k4 + 1],
                          gqk[0:n, l, k4, :], ALU.mult, ALU.mult)
                P.copy("act", vs[0:n, :], ps[0:n, 256:384])
                P.copy("dve", VA[0:n, ti, :, 0:64], ps[0:n, 256:384].rearrange("p (a b) -> p a b", a=2))
                P.dma("sp", dstk[t0:t0 + n, p * 128:(p + 1) * 128], qk[0:n, 2:4, :].rearrange("p a b -> p (a b)"))
                P.dma("sp", dstv[t0:t0 + n, p * 128:(p + 1) * 128], vs[0:n, :])
                return (qk, t0, n)

            def projB(st):
                qk, t0, n = st
                pt_ = TRB[cnt["tp"] % 2]
                cnt["tp"] += 1
                P.mm(pt_[:, 0:n], qk[0:n, 0:2, :].rearrange("p a b -> p (a b)"), ident32[0:n, 0:n])
                P.mm(pt_[:, 128:128 + n], qk[0:n, 2:4, :].rearrange("p a b -> p (a b)"), ident32[0:n, 0:n])
                P.copy("act", QT[:, t0:t0 + n], pt_[:, 0:n])
                P.copy("act", KT[:, t0:t0 + n], pt_[:, 128:128 + n])

            pend = []
            for ti, (t0, n) in enumerate(seq.ktiles):
                pend.append(projA(ti, t0, n))
                if len(pend) > 2:
                    projB(pend.pop(0))
            while pend:
                projB(pend.pop(0))
            if p + 2 < 4:
                load_qkv(p + 2)
            if ATT_LEVEL < 4:
                continue
            items = []
            if sample:
                for (gi, t0, nq) in qtiles:
                    for j in range(gi + 1):
                        items.append(("diag", gi, t0, nq, j, 0, gi))
            else:
                for I in range(4):
                    for j in range(4 * I):
                        items.append(("off", 4 * I, 512 * I, 512, j, 0, 4 * I - 1))
                    for i in range(4 * I, 4 * I + 4):
                        for j in range(4 * I, i + 1):
                            items.append(("diag", i, 128 * i, 128, j, 4 * I, i))
            state = {"kgrp": None, "bgi": None}
            groups = {}

            def prep_dma(g):
                kb, v32 = kraw[g % 3], vraw32[g % 3]
                j0 = g * 4
                P.dma("sp", kb, ck[l, j0 * 128:(j0 + 4) * 128, p * 128:(p + 1) * 128]
                      .rearrange("(j p) c -> p j c", p=128))
                P.dma("sp", v32, cv[l, j0 * 128:(j0 + 4) * 128, p * 128:(p + 1) * 128]
                      .rearrange("(j p) c -> p j c", p=128))

            def prep_tr(g):
                kb, v32, vb, ktg = kraw[g % 3], vraw32[g % 3], vraw[g % 3], KTg[g % 2]
                for q4 in range(4):
                    P.mm(pb[1][:, q4 * 128:(q4 + 1) * 128], kb[:, q4, :], ident32)
                P.copy("dve", ktg, pb[1].rearrange("p (a b) -> p a b", a=4))
                eb = state["eb"]
                for q4 in range(4):
                    for hh in range(2):
                        sc = eb[:, hh, 4 * g + q4:4 * g + q4 + 1]
                        P.act(vb[:, q4, hh, 0:64], v32[:, q4, hh * 64:(hh + 1) * 64], AF.Copy, scale=sc)
                        P.ts("dve", vb[:, q4, hh, 64:128], ones32[:, 0:64], sc, None, ALU.mult)
                groups[g] = (ktg, vb)

            def scores(it):
                kind, gi, t0, nq, j, jf, jl_ = it
                if state["bgi"] != gi:
                    bb = biasb[cnt["b"] % 2]
                    cnt["b"] += 1
                    for hh in range(2):
                        P.ts("dve", bb[:, hh, 0:gi + 1], ctok[:, 2 * p + hh, 0:gi + 1], -1.0,
                             rall[:, gi, 2 * p + hh:2 * p + hh + 1], ALU.mult, ALU.add)
                    state["bb"] = bb
                    state["bgi"] = gi
                if j == jf:
                    if kind == "off":
                        state["po"] = None
                    else:
                        state["po"] = pb[6 + cnt["o"] % 2].rearrange("p (a b) -> p a b", a=4)
                        cnt["o"] += 1
                bb, po = state["bb"], state["po"]
                psS = [pb[2 + cnt["s"] % 2], pb[4 + cnt["s"] % 2]]
                cnt["s"] += 1
                ptile = PT[cnt["pt"] % 3]
                cnt["pt"] += 1
                diag = (kind == "diag" and j == gi)
                if j < NP:
                    g = j // 4
                    if j % 4 == 0:
                        ng = NP // 4
                        if g == 0:
                            prep_dma(0)
                            prep_dma(1)
                            prep_tr(0)
                        if g + 2 < ng:
                            prep_dma(g + 2)
                        if g + 1 < ng:
                            prep_tr(g + 1)
                    ktg, vb = groups[g]
                    nk = 128
                    kT = [ktg[hh * 64:(hh + 1) * 64, j % 4, :] for hh in range(2)]
                    vv = [vb[:, j % 4, hh, :] for hh in range(2)]
                else:
                    jj = j - NP
                    k0, nk = seq.ktiles[jj]
                    kT = [KT[hh * 64:(hh + 1) * 64, k0:k0 + nk] for hh in range(2)]
                    vv = [VA[0:nk, jj, hh, :] for hh in range(2)]
                for hh in range(2):
                    P.mm(psS[hh][0:nk, 0:nq], kT[hh], QT[hh * 64:(hh + 1) * 64, t0:t0 + nq],
                         start=True, stop=not diag)
                if diag:
                    for hh in range(2):
                        P.mm(psS[hh][0:nk, 0:nq], identb[:, 0:nk], maskb[:, 0:nq], start=False, stop=True)
                return dict(it=it, psS=psS, ptile=ptile, nk=nk, vv=vv, bb=bb, po=po)

            def exps(d):
                kind, gi, t0, nq, j, jf, jl_ = d["it"]
                nk = d["nk"]
                for hh in range(2):
                    P.act(d["ptile"][0:nk, hh, 0:nq], d["psS"][hh][0:nk, 0:nq], AF.Exp,
                          bias=d["bb"][0:nk, hh, j:j + 1], scale=0.125)

            def pv(d):
                kind, gi, t0, nq, j, jf, jl_ = d["it"]
                nk, po = d["nk"], d["po"]
                if kind == "off":
                    for hh in range(2):
                        P.mm(pb[hh][:, 0:nq], d["vv"][hh], d["ptile"][0:nk, hh, 0:nq], start=(j == jf), stop=(j == jl_))
                    return
                for hh in range(2):
                    P.mm(po[:, hh, 0:nq], d["vv"][hh], d["ptile"][0:nk, hh, 0:nq], start=(j == jf and hh == 0),
                         stop=(j == jl_), skip=True)
                if j == jl_:
                    rc = rec[cnt["r"] % 2]
                    cnt["r"] += 1
                    src = po
                    if (not sample) and gi >= 4:
                        tf = toff[cnt["r"] % 2]
                        sub = (gi % 4) * 128
                        for hh in range(2):
                            P.ts("dve", tf[:, hh, :], pb[hh][:, sub:sub + 128],
                                 alpha[:, gi, 2 * p + hh:2 * p + hh + 1], None, ALU.mult)
                        P.tt("dve", tf[:, :, 0:nq], tf[:, :, 0:nq], po[:, 0:2, 0:nq], ALU.add)
                        src = tf
                    P.op("dve", lambda e, o=rc[64:128, :, 0:nq], i=src[64:128, 0:2, 0:nq]: e.reciprocal(o, i),
                         reads=[src[64:128, 0:2, 0:nq]], writes=[rc[64:128, :, 0:nq]])
                    P.copy("dve", rc[0:64, :, 0:nq], rc[64:128, :, 0:nq])
                    cs = seq.col0 + t0
                    P.tt("dve", cat[0:64, 4 + p, cs:cs + nq], src[0:64, 0, 0:nq], rc[0:64, 0, 0:nq], ALU.mult)
                    P.tt("dve", rc[0:64, 0, 0:nq], src[0:64, 1, 0:nq], rc[0:64, 1, 0:nq], ALU.mult)
                    P.copy("dve", cat[64:128, 4 + p, cs:cs + nq], rc[0:64, 0, 0:nq])

            if sample:
                gi, t0, nq = qtiles[0]
                bb = biasb[cnt["b"] % 2]
                cnt["b"] += 1
                ebt = ebb[p % 2]
                for hh in range(2):
                    P.ts("dve", bb[:, hh, 0:gi + 1], ctok[:, 2 * p + hh, 0:gi + 1], -1.0,
                         rall[:, gi, 2 * p + hh:2 * p + hh + 1], ALU.mult, ALU.add)
                P.act(ebt[:, :, 0:NP], bb[:, :, 0:NP], AF.Exp)
                state["eb"] = ebt
                state["bb"] = bb
                state["bgi"] = gi
                po = pb[6 + cnt["o"] % 2].rearrange("p (a b) -> p a b", a=4)
                cnt["o"] += 1
                state["po"] = po
                ng = NP // 4
                prep_dma(0)
                prep_dma(1)
                prep_tr(0)

                def g_scores(g):
                    if g + 2 < ng:
                        prep_dma(g + 2)
                    if g + 1 < ng:
                        prep_tr(g + 1)
                    ktg, vb = groups[g]
                    psS = [pb[2 + cnt["s"] % 2], pb[4 + cnt["s"] % 2]]
                    cnt["s"] += 1
                    ptile = PT[cnt["pt"] % 3]
                    cnt["pt"] += 1
                    for q4 in range(4):
                        for hh in range(2):
                            P.mm(psS[hh][:, q4 * 32:q4 * 32 + nq], ktg[hh * 64:(hh + 1) * 64, q4, :],
                                 QT[hh * 64:(hh + 1) * 64, t0:t0 + nq], start=True, stop=True)
                    return (g, psS, ptile, vb)

                def g_exps(d):
                    g, psS, ptile, vb = d
                    for hh in range(2):
                        P.act(ptile[:, hh, 0:128], psS[hh][:, 0:128], AF.Exp, scale=0.125)

                def g_pv(d):
                    g, psS, ptile, vb = d
                    for q4 in range(4):
                        for hh in range(2):
                            P.mm(po[:, hh, 0:nq], vb[:, q4, hh, :], ptile[:, hh, q4 * 32:q4 * 32 + nq],
                                 start=(g == 0 and q4 == 0 and hh == 0), stop=False, skip=True)

                prevg = None
                for g in range(ng):
                    d = g_scores(g)
                    if prevg is not None:
                        g_pv(prevg)
                    g_exps(d)
                    prevg = d
                it = ("diag", gi, t0, nq, gi, 0, gi)
                dd = scores(it)
                g_pv(prevg)
                exps(dd)
                pv(dd)
            else:
                prevd = None
                for it in items:
                    d = scores(it)
                    if prevd is not None:
                        pv(prevd)
                    exps(d)
                    prevd = d
                pv(prevd)

    def wout_part(seqs, l, next_norm=None):
        A.reset()
        wo = A.alloc([128, NCH, D], BF16)
        P.dma("pool", wo, W["w_out"][l].rearrange("(c p) d -> p c d", p=128))
        nb = norm_bufs(A) if next_norm is not None else None
        tiles = [tt for s in seqs for tt in s.ttiles]
        k = 0
        prevt = None
        for (c0, n) in tiles:
            for dc in range(NCH):
                ps = pb[k % 2]
                k += 1
                for kc in range(NCH):
                    P.mm(ps[:, 0:n], wo[:, kc, dc * 128:(dc + 1) * 128], cat[:, kc, c0:c0 + n],
                         start=(kc == 0), stop=(kc == NCH - 1))
                P.tt("dve", x[:, dc, c0:c0 + n], ps[:, 0:n], x[:, dc, c0:c0 + n], ALU.add)
            if next_norm is not None:
                if prevt is not None:
                    norm_tile(next_norm[0], next_norm[1], prevt[0], prevt[1], nb)
                prevt = (c0, n)
        if next_norm is not None:
            norm_tile(next_norm[0], next_norm[1], prevt[0], prevt[1], nb)

    ORDER = ["load", "ffn1", "norm", "pool", "conv", "att", "wout", "ffn2"]
    def upto(name, l):
        if stage is None:
            return True
        sl, sn = stage
        if l < sl:
            return True
        if l > sl:
            return False
        return ORDER.index(name) <= ORDER.index(sn)
    passes = [[Seq("prompt", 0, SEQ, 0)], [Seq("prompt", 0, SEQ, 1), Seq("sample", SEQ, DEC, 0)]][:npass]
    for seqs in passes:
        A.reset()
        for s in seqs:
            load_x(s)
        for l in range(2):
            full = stage is None
            if upto("ffn1", l):
                ffn(seqs, l, "ffn1", prenormed=(full and l > 0), next_norm=("mix_norm", l) if full else None)
            if upto("norm", l) and not full:
                rmsnorm(seqs, "mix_norm", l)
            for s in seqs:
                gen = att_prologue(s, l) if upto("att", l) else iter(())
                if s.kind == "sample":
                    for _ in gen:
                        pass
                if upto("pool", l):
                    pool_part(s, l)
                if upto("conv", l):
                    conv_part(s, l, hook=lambda g=gen: next(g, None))
                for _ in gen:
                    pass
                if upto("att", l):
                    att_part(s, l)
            if upto("wout", l):
                wout_part(seqs, l, next_norm=("ffn2_norm", l) if full else None)
            if upto("ffn2", l):
                ffn(seqs, l, "ffn2", prenormed=full,
                    next_norm=("ffn1_norm", l + 1) if (full and l + 1 < 2) else None)
        A.reset()
        for s in seqs:
            store_y(s)
    P.emit()
    return nc


_NC_CACHE = {}


def _consts():
    ident = np.eye(128, dtype=np.float32)
    tri = np.triu(np.ones((128, 128), dtype=np.float32))
    kk = np.arange(128)[:, None]
    qq = np.arange(128)[None, :]
    mask = np.where(kk > qq, -30000.0, 0.0).astype(np.float32)
    inv = np.zeros((128, 2, 16), dtype=np.float32)
    wins = (2, 4, 8, 16)
    for cc in range(2):
        for p in range(128):
            w = wins[2 * cc + p // 64]
            for t in range(16):
                inv[p, cc, t] = 1.0 / min(t + 1, w)
    return ident, tri, mask, inv


def kernel(**inputs):
    if "nc" not in _NC_CACHE:
        _NC_CACHE["nc"] = build_program()
    nc = _NC_CACHE["nc"]
    f = lambda a: np.ascontiguousarray(np.asarray(a, dtype=np.float32))
    ident, tri, mask, inv = _consts()
    wnames = ["ffn1_norm", "ffn1_w_gu", "ffn1_w_down", "mix_norm", "w_in", "w_out", "pool_w", "pool_scale",
              "conv_w", "conv_b", "conv_ln_g", "conv_ln_b", "q_norm", "k_norm", "forget_b",
              "ffn2_norm", "ffn2_w_gu", "ffn2_w_down"]
    shared = {k: f(inputs[k]) for k in wnames}
    shared.update(c_ident=ident, c_tri=tri, c_mask=mask, c_invcnt=inv)
    xpr, xsm = f(inputs["x_prompt"]), f(inputs["x_sample"])
    sp_, sc_ = f(inputs["state_pool"]), f(inputs["state_conv"])
    ck_, cv_, cl_ = f(inputs["cache_k"]), f(inputs["cache_v"]), f(inputs["cache_logf"])
    in_maps = []
    for c in range(8):
        m = dict(shared)
        m["xp"] = np.ascontiguousarray(xpr[2 * c:2 * c + 2])
        m["xs"] = np.ascontiguousarray(xsm[c])
        m["spool"] = np.ascontiguousarray(sp_[:, c])
        m["sconv"] = np.ascontiguousarray(sc_[:, c])
        m["ck"] = np.ascontiguousarray(ck_[:, c].reshape(2, PAST, 512))
        m["cv"] = np.ascontiguousarray(cv_[:, c].reshape(2, PAST, 512))
        m["clf"] = np.ascontiguousarray(cl_[:, c])
        in_maps.append(m)
    res = run_bass_kernel_spmd(nc, in_maps, core_ids=list(range(8)))
    R = res.results
    cat0 = lambda k: np.concatenate([r[k] for r in R], axis=0)
    cat1 = lambda k: np.concatenate([r[k] for r in R], axis=1)
    st1 = lambda k: np.stack([r[k] for r in R], axis=1)
    y_prompt = cat0("y_p")
    y_sample = np.stack([r["y_s"] for r in R], axis=0)
    outs = (
        y_prompt, y_sample,
        cat1("npool_p"), st1("npool_s"),
        cat1("nconv_p"), st1("nconv_s"),
        cat1("nk_p").reshape(2, 16, SEQ, 8, 64), cat1("nv_p").reshape(2, 16, SEQ, 8, 64), cat1("nf_p"),
        st1("nk_s").reshape(2, 8, DEC, 8, 64), st1("nv_s").reshape(2, 8, DEC, 8, 64), st1("nf_s"),
    )
    return tuple(np.ascontiguousarray(o, dtype=np.float32) for o in outs)
```

```python
import numpy as np
import concourse.bass as bass
import concourse.mybir as mybir
from concourse.bass_utils import run_bass_kernel_spmd

F32 = mybir.dt.float32
BF16 = mybir.dt.bfloat16
AF = mybir.ActivationFunctionType
ALU = mybir.AluOpType
AX = mybir.AxisListType

_DT_SIZE = {F32: 4, BF16: 2}


class Op:
    __slots__ = ("eng", "fn", "deps", "is_dma", "signal", "sem", "val", "idx", "seq")

    def __init__(self, eng, fn, is_dma):
        self.eng = eng
        self.fn = fn
        self.deps = []
        self.is_dma = is_dma
        self.signal = False
        self.sem = None
        self.val = 0


class Prog:
    ENGS = ("pe", "act", "dve", "pool", "sp")

    def __init__(self, nc, n_dma_sems=14):
        self.nc = nc
        self.ops = {e: [] for e in self.ENGS}
        self.acc = {}
        self.n_dma_sems = n_dma_sems
        self.tracked = set()
        self.nbuf = 0
        self.waited = {e: {} for e in self.ENGS}
        self.psum = set()

    def sb(self, name, shape, dt):
        t = self.nc.alloc_sbuf_tensor(name, list(shape), dt)
        self.tracked.add(name)
        return t.ap()

    def ps(self, name, shape, dt=F32):
        t = self.nc.alloc_psum_tensor(name, list(shape), dt)
        self.tracked.add(name)
        self.psum.add(name)
        return t.ap()

    def _regions(self, ap):
        name = ap.tensor.name
        if name not in self.tracked:
            return ()
        pat = ap.ap
        off = ap.offset
        pstride = pat[0][0]
        if pstride <= 0:
            pstride = 1 << 40
        p0 = off // pstride
        f0 = off % pstride
        p1 = p0 + pat[0][1]
        sz = _DT_SIZE.get(ap.dtype, 4)
        free = list(pat[1:])
        outer = [(0, 1)]
        if len(free) >= 2 and free[0][1] <= 40 and free[0][0] > 0:
            outer = [(free[0][0], free[0][1])]
            free = free[1:]
        ext = 0
        for st, cnt in free:
            ext += abs(st) * (cnt - 1)
        res = []
        ost, ocnt = outer[0]
        for i in range(ocnt):
            a = f0 + i * ost
            res.append((name, p0, p1, a * sz, (a + ext + 1) * sz))
        return res

    def op(self, eng, fn, reads=(), writes=(), dma=False):
        o = Op(eng, fn, dma)
        deps = set()
        rregs = []
        wregs = []
        for ap in reads:
            rregs.extend(self._regions(ap))
        for ap in writes:
            wregs.extend(self._regions(ap))
        for (name, p0, p1, f0, f1) in rregs:
            lst = self.acc.get(name)
            if not lst:
                continue
            if name in self.psum:
                for (q0, q1, g0, g1, po, w) in lst:
                    if po.eng != eng or (w and q0 < p1 and p0 < q1 and g0 < f1 and f0 < g1):
                        deps.add(po)
                continue
            for (q0, q1, g0, g1, po, w) in lst:
                if w and q0 < p1 and p0 < q1 and g0 < f1 and f0 < g1:
                    deps.add(po)
        for (name, p0, p1, f0, f1) in wregs:
            lst = self.acc.get(name)
            if not lst:
                continue
            keep = []
            isps = name in self.psum
            for ent in lst:
                (q0, q1, g0, g1, po, w) = ent
                if isps and po.eng != eng:
                    deps.add(po)
                if q0 < p1 and p0 < q1 and g0 < f1 and f0 < g1:
                    deps.add(po)
                    if q0 >= p0 and q1 <= p1 and g0 >= f0 and g1 <= f1:
                        continue
                keep.append(ent)
            self.acc[name] = keep
        for (name, p0, p1, f0, f1) in rregs:
            self.acc.setdefault(name, []).append((p0, p1, f0, f1, o, False))
        for (name, p0, p1, f0, f1) in wregs:
            self.acc.setdefault(name, []).append((p0, p1, f0, f1, o, True))
        deps.discard(o)
        best = {}
        for d in deps:
            if d.is_dma:
                o.deps.append(d)
                continue
            if d.eng == "pe" and eng == "pe":
                continue
            b = best.get(d.eng)
            if b is None or d.seq > b.seq:
                best[d.eng] = d
        for d in best.values():
            w = self.waited[eng].get(d.eng, -1)
            if d.seq <= w:
                continue
            self.waited[eng][d.eng] = d.seq
            o.deps.append(d)
            d.signal = True
        o.seq = len(self.ops[eng])
        self.ops[eng].append(o)
        return o

    def mm(self, out, lhsT, rhs, start=True, stop=True, skip=False):
        if skip:
            return self.op("pe", lambda e: e.matmul(out, lhsT, rhs, start=start, stop=stop, skip_group_check=True),
                           reads=[lhsT, rhs], writes=[out])
        return self.op("pe", lambda e: e.matmul(out, lhsT, rhs, start=start, stop=stop),
                       reads=[lhsT, rhs], writes=[out])

    def act(self, out, in_, func, bias=None, scale=None, eng="act", extra_reads=()):
        kw = {}
        rd = [in_] + list(extra_reads)
        if bias is not None:
            kw["bias"] = bias
            if not isinstance(bias, (int, float)):
                rd.append(bias)
        if scale is not None:
            kw["scale"] = scale
            if not isinstance(scale, (int, float)):
                rd.append(scale)
        return self.op("act", lambda e: e.activation(out, in_, func, **kw), reads=rd, writes=[out])

    def tt(self, eng, out, in0, in1, op):
        return self.op(eng, lambda e: e.tensor_tensor(out, in0, in1, op), reads=[in0, in1], writes=[out])

    def ts(self, eng, out, in0, s1, s2, op0, op1=None):
        rd = [in0]
        if not isinstance(s1, (int, float)):
            rd.append(s1)
        if s2 is not None and not isinstance(s2, (int, float)):
            rd.append(s2)
        if op1 is None:
            return self.op(eng, lambda e: e.tensor_scalar(out, in0, s1, None, op0), reads=rd, writes=[out])
        return self.op(eng, lambda e: e.tensor_scalar(out, in0, s1, s2, op0, op1), reads=rd, writes=[out])

    def stt(self, eng, out, in0, scalar, in1, op0, op1):
        rd = [in0, in1]
        if not isinstance(scalar, (int, float)):
            rd.append(scalar)
        return self.op(eng, lambda e: e.scalar_tensor_tensor(out, in0, scalar, in1, op0, op1),
                       reads=rd, writes=[out])

    def copy(self, eng, out, in_):
        if eng == "act":
            return self.op("act", lambda e: e.copy(out, in_), reads=[in_], writes=[out])
        return self.op(eng, lambda e: e.tensor_copy(out, in_), reads=[in_], writes=[out])

    def memset(self, eng, out, val):
        return self.op(eng, lambda e: e.memset(out, val), writes=[out])

    def dma(self, q, out, in_):
        return self.op(q, lambda e: e.dma_start(out, in_), reads=[in_], writes=[out], dma=True)

    def emit(self):
        nc = self.nc
        esem = {e: nc.alloc_semaphore("S_" + e) for e in self.ENGS}
        dsem = {e: [nc.alloc_semaphore("D_%s_%d" % (e, i)) for i in range(self.n_dma_sems)]
                for e in ("pool", "sp", "act")}
        dcnt = {e: [0] * self.n_dma_sems for e in dsem}
        drr = {e: 0 for e in dsem}
        final_waits = []
        for e in self.ENGS:
            if self.ops[e]:
                self.ops[e][-1].signal = True
        for e in self.ENGS:
            cnt = 0
            for o in self.ops[e]:
                if o.is_dma:
                    k = drr[e]
                    drr[e] = (k + 1) % self.n_dma_sems
                    dcnt[e][k] += 16
                    o.sem = dsem[e][k]
                    o.val = dcnt[e][k]
                    o.signal = True
                elif o.signal:
                    cnt += 1
                    o.sem = esem[e]
                    o.val = cnt
        for e in dsem:
            for k in range(self.n_dma_sems):
                if dcnt[e][k]:
                    final_waits.append((dsem[e][k], dcnt[e][k]))
        ops = self.ops

        def run(engname, eobj, last=False):
            waited = {}
            for o in ops[engname]:
                need = {}
                for d in o.deps:
                    key = d.sem.num
                    if need.get(key, (None, 0))[1] < d.val:
                        need[key] = (d.sem, d.val)
                if o.is_dma and o.val > 16:
                    key = o.sem.num
                    if need.get(key, (None, 0))[1] < o.val - 16:
                        need[key] = (o.sem, o.val - 16)
                for key, (s, v) in need.items():
                    if waited.get(key, 0) >= v:
                        continue
                    eobj.wait_ge(s, v)
                    waited[key] = v
                ins = o.fn(eobj)
                if o.signal:
                    ins.then_inc(o.sem, 16 if o.is_dma else 1)
            if last:
                for s, v in final_waits:
                    if waited.get(s.num, 0) < v:
                        eobj.wait_ge(s, v)
                for e2 in self.ENGS:
                    if e2 == engname:
                        continue
                    sig = [o for o in ops[e2] if o.signal and not o.is_dma]
                    if sig:
                        eobj.wait_ge(esem[e2], sig[-1].val)

        with nc.Block() as block:
            @block.tensor
            def _(e):
                run("pe", e)

            @block.scalar
            def _(e):
                run("act", e)

            @block.vector
            def _(e):
                run("dve", e)

            @block.gpsimd
            def _(e):
                run("pool", e)

            @block.sync
            def _(e):
                run("sp", e, last=True)
D = 1024
NCH = 8
SEQ = 2048
DEC = 32
PAST = 4096
DFF = 2816
NFC = 22
DIN = 2312
EPS = 1e-6
TTMAX = SEQ + DEC
ARENA_BYTES = 66 * 1024
FGROUPS = [(0, 6), (6, 6), (12, 5), (17, 5)]
import os as _os
ATT_LEVEL = int(_os.environ.get('ATT_LEVEL', '4'))


class Arena:
    def __init__(self, ap_bf16, nbytes):
        self.ap = ap_bf16
        self.nbytes = nbytes
        self.off = 0

    def reset(self):
        self.off = 0

    def alloc(self, shape, dt):
        sz = _DT_SIZE[dt]
        n = 1
        for s in shape[1:]:
            n *= s
        nb = (n * sz + 63) // 64 * 64
        assert self.off + nb <= self.nbytes, ("arena overflow", self.off, nb, shape)
        v = self.ap[:, self.off // 2:(self.off + n * sz) // 2]
        self.off += nb
        if dt != BF16:
            v = v.bitcast(dt)
        if len(shape) == 3:
            v = v.rearrange("p (a b) -> p a b", a=shape[1])
        elif len(shape) == 4:
            v = v.rearrange("p (a b c) -> p a b c", a=shape[1], b=shape[2])
        if shape[0] < 128:
            v = v[0:shape[0]]
        return v


class Seq:
    def __init__(self, kind, col0, T, b):
        self.kind = kind
        self.col0 = col0
        self.T = T
        self.b = b
        self.ttiles = []
        t = 0
        while t < T:
            n = min(512, T - t)
            self.ttiles.append((col0 + t, n))
            t += n
        self.ktiles = []
        t = 0
        while t < T:
            n = min(128, T - t)
            self.ktiles.append((t, n))
            t += n


def build_program(stage=None, npass=2):
    nc = bass.Bass("TRN2", target_bir_lowering=False)
    P = Prog(nc)

    def din(name, shape):
        return nc.dram_tensor(name, list(shape), F32, kind="ExternalInput").ap()

    def dout(name, shape):
        return nc.dram_tensor(name, list(shape), F32, kind="ExternalOutput").ap()

    xp = din("xp", [2, SEQ, D])
    xs = din("xs", [DEC, D])
    spool = din("spool", [2, 15, 256])
    sconv = din("sconv", [2, 30, 256])
    ck = din("ck", [2, PAST, 512])
    cv = din("cv", [2, PAST, 512])
    clf = din("clf", [2, PAST, 8])
    W = {}
    for nm, shp in [("ffn1_norm", [2, D]), ("ffn1_w_gu", [2, D, 2 * DFF]), ("ffn1_w_down", [2, DFF, D]),
                    ("mix_norm", [2, D]), ("w_in", [2, D, DIN]), ("w_out", [2, D, D]),
                    ("pool_w", [2, 4, 64, 64]), ("pool_scale", [2, 256]), ("conv_w", [2, 31, 256]),
                    ("conv_b", [2, 256]), ("conv_ln_g", [2, 256]), ("conv_ln_b", [2, 256]),
                    ("q_norm", [2, 64]), ("k_norm", [2, 64]), ("forget_b", [2, 8]),
                    ("ffn2_norm", [2, D]), ("ffn2_w_gu", [2, D, 2 * DFF]), ("ffn2_w_down", [2, DFF, D])]:
        W[nm] = din(nm, shp)
    c_ident = din("c_ident", [128, 128])
    c_tri = din("c_tri", [128, 128])
    c_mask = din("c_mask", [128, 128])
    c_invcnt = din("c_invcnt", [128, 2, 16])

    y_p = dout("y_p", [2, SEQ, D])
    y_s = dout("y_s", [DEC, D])
    npool_p = dout("npool_p", [2, 2, 15, 256])
    npool_s = dout("npool_s", [2, 15, 256])
    nconv_p = dout("nconv_p", [2, 2, 30, 256])
    nconv_s = dout("nconv_s", [2, 30, 256])
    nk_p = dout("nk_p", [2, 2, SEQ, 512])
    nv_p = dout("nv_p", [2, 2, SEQ, 512])
    nf_p = dout("nf_p", [2, 2, SEQ, 8])
    nk_s = dout("nk_s", [2, DEC, 512])
    nv_s = dout("nv_s", [2, DEC, 512])
    nf_s = dout("nf_s", [2, DEC, 8])

    x = P.sb("x", [128, NCH, TTMAX], F32)
    xn = P.sb("xn", [128, NCH, TTMAX], BF16)
    cat = P.sb("cat", [128, NCH, TTMAX], BF16)
    arena_ap = P.sb("arena", [128, ARENA_BYTES // 2], BF16)
    A = Arena(arena_ap, ARENA_BYTES)
    ident32 = P.sb("ident32", [128, 128], F32)
    identb = P.sb("identb", [128, 128], BF16)
    tri32 = P.sb("tri32", [128, 128], F32)
    ones32 = P.sb("ones32", [128, 128], F32)
    onesb = P.sb("onesb", [128, 128], BF16)
    maskb = P.sb("maskb", [128, 128], BF16)
    invcnt = P.sb("invcnt", [128, 2, 16], F32)
    epsT = P.sb("epsT", [128, 1], F32)
    prm = P.sb("prm", [128, 64], F32)
    cwT = P.sb("cwT", [128, 124], F32)
    gqk = P.sb("gqk", [128, 2, 4, 64], F32)
    fbT = P.sb("fbT", [128, 2, 8], F32)
    pwb = P.sb("pwb", [128, 2, 2, 128], BF16)
    pb = [P.ps("pb%d" % i, [128, 512], F32) for i in range(8)]
    wf = P.sb("wf", [128, NCH, 8], BF16)
    lf = P.sb("lf", [128, 33, 8], F32)
    ua = P.sb("ua", [128, 33, 8], F32)
    ctok = P.sb("ctok", [128, 8, 33], F32)
    rall = P.sb("rall", [128, 33, 8], F32)
    Sp = [P.sb("Sp%d" % i, [128, 8], F32) for i in range(2)]
    alpha = P.sb("alpha", [128, 16, 8], F32)

    for t_ in (lf, ua, ctok, rall, alpha, Sp[0], Sp[1]):
        P.memset("dve", t_, 0.0)
    P.dma("sp", ident32, c_ident)
    P.dma("sp", tri32, c_tri)
    P.dma("pool", identb, c_ident)
    P.dma("pool", maskb, c_mask)
    P.dma("sp", invcnt, c_invcnt)
    P.memset("dve", ones32, 1.0)
    P.memset("dve", onesb, 1.0)
    P.memset("dve", epsT, EPS)
    A.reset()
    rows = A.alloc([64, 128], F32)
    rows2 = A.alloc([124, 128], F32)
    r = 0
    PRM = {}
    for nm, nr in [("ffn1_norm", 16), ("mix_norm", 16), ("ffn2_norm", 16), ("pool_scale", 4),
                   ("conv_b", 4), ("conv_ln_g", 4), ("conv_ln_b", 4)]:
        P.dma("sp", rows[r:r + nr, :], W[nm].rearrange("l (c p) -> (l c) p", p=128))
        PRM[nm] = r
        r += nr
    P.dma("sp", rows2, W["conv_w"].rearrange("l j (c p) -> (l j c) p", p=128))
    P.mm(pb[7][:, 0:64], rows, ident32[0:64, 0:64])
    P.copy("dve", prm, pb[7][:, 0:64])
    P.mm(pb[7][:, 128:252], rows2, ident32[0:124, 0:124])
    P.copy("dve", cwT, pb[7][:, 128:252])
    for l in range(2):
        for k in range(4):
            src = W["q_norm"] if k < 2 else W["k_norm"]
            P.dma("sp", gqk[:, l, k, :], src[l:l + 1, :].partition_broadcast(128))
        P.dma("sp", fbT[:, l, :], W["forget_b"][l:l + 1, :].partition_broadcast(128))
    P.memset("dve", pwb, 0.0)
    for l in range(2):
        for cc in range(2):
            for g2 in range(2):
                P.dma("pool", pwb[g2 * 64:(g2 + 1) * 64, l, cc, g2 * 64:(g2 + 1) * 64],
                      W["pool_w"][l, 2 * cc + g2])

    def gvec(nm, l, c):
        base = PRM[nm]
        nper = 8 if nm.endswith("norm") else 2
        k = base + l * nper + c
        return prm[:, k:k + 1]

    def load_x(seq):
        src = xp[seq.b] if seq.kind == "prompt" else xs
        for (t0, n) in seq.ktiles:
            xt = A.alloc([128, D], F32)
            P.dma("sp", xt[0:n, :], src[t0:t0 + n, :])
            for half in range(2):
                ps = pb[6 + half]
                for q in range(4):
                    dc = half * 4 + q
                    P.mm(ps[:, q * 128:q * 128 + n], xt[0:n, dc * 128:(dc + 1) * 128], ident32[0:n, 0:n])
                dst = x[:, half * 4:half * 4 + 4, seq.col0 + t0:seq.col0 + t0 + n]
                srcp = ps.rearrange("p (a b) -> p a b", a=4)[:, :, 0:n]
                P.copy("act" if half == 0 else "dve", dst, srcp)
            if A.off > ARENA_BYTES - 8192:
                A.reset()

    def store_y(seq):
        dst = y_p[seq.b] if seq.kind == "prompt" else y_s
        for (t0, n) in seq.ktiles:
            yt = A.alloc([128, D], F32)
            for half in range(2):
                ps = pb[6 + half]
                for q in range(4):
                    dc = half * 4 + q
                    P.mm(ps[0:n, q * 128:(q + 1) * 128], x[:, dc, seq.col0 + t0:seq.col0 + t0 + n], ident32)
                P.copy("act" if half == 0 else "dve", yt[0:n, half * 512:(half + 1) * 512], ps[0:n, :])
            P.dma("sp", dst[t0:t0 + n, :], yt[0:n, :])
            if A.off > ARENA_BYTES - 8192:
                A.reset()

    def norm_bufs(arena):
        sqs = [arena.alloc([128, NCH, 512], BF16) for _ in range(2)]
        sds = [arena.alloc([128, 512], F32) for _ in range(2)]
        rss = [arena.alloc([128, 512], F32) for _ in range(2)]
        return {"sq": sqs, "sd": sds, "rs": rss, "k": 0}

    def norm_tile(nm, l, c0, n, nb):
        k_ = nb["k"]
        nb["k"] += 1
        sq, sd, rs = nb["sq"][k_ % 2], nb["sd"][k_ % 2], nb["rs"][k_ % 2]
        P.act(sq[:, :, 0:n], x[:, :, c0:c0 + n], AF.Square)
        for c in range(NCH):
            P.mm(pb[6][:, 0:n], onesb, sq[:, c, 0:n], start=(c == 0), stop=(c == NCH - 1))
        P.act(sd[:, 0:n], pb[6][:, 0:n], AF.Sqrt, bias=epsT, scale=1.0 / D)
        P.op("dve", lambda e, o=rs[:, 0:n], i=sd[:, 0:n]: e.reciprocal(o, i),
             reads=[sd[:, 0:n]], writes=[rs[:, 0:n]])
        for c in range(NCH):
            P.stt("dve", xn[:, c, c0:c0 + n], x[:, c, c0:c0 + n], gvec(nm, l, c), rs[:, 0:n],
                  ALU.mult, ALU.mult)

    def rmsnorm(seqs, nm, l):
        A.reset()
        nb = norm_bufs(A)
        for seq in seqs:
            for (c0, n) in seq.ttiles:
                norm_tile(nm, l, c0, n, nb)

    def ffn(seqs, l, which, prenormed=False, next_norm=None):
        if not prenormed:
            rmsnorm(seqs, which + "_norm", l)
        wgu_d = W[which + "_w_gu"][l].rearrange("(c p) f -> p c f", p=128)
        wd_d = W[which + "_w_down"][l].rearrange("(f p) d -> p f d", p=128)
        A.reset()
        g = A.alloc([128, 6, TTMAX], BF16)
        off_wgu = A.off
        wgu = [A.alloc([128, NCH, 256], BF16) for _ in range(3)]
        wd = [A.alloc([128, 6, D], BF16) for _ in range(2)]
        st = [A.alloc([128, 512], F32) for _ in range(2)]
        tiles = [tt for s in seqs for tt in s.ttiles]

        def load_gu(fc):
            b = wgu[fc % 3]
            P.dma("pool", b[:, :, 0:128], wgu_d[:, :, fc * 128:(fc + 1) * 128])
            P.dma("pool", b[:, :, 128:256], wgu_d[:, :, DFF + fc * 128:DFF + (fc + 1) * 128])

        def load_d(gi):
            f0, nf = FGROUPS[gi]
            P.dma("pool", wd[gi % 2][:, 0:nf, :], wd_d[:, f0:f0 + nf, :])

        for fc in range(3):
            load_gu(fc)
        load_d(0)
        load_d(1)
        it = 0
        cntd = {"ky": 0}
        for gi, (f0, nf) in enumerate(FGROUPS):
            for fl in range(nf):
                fc = f0 + fl
                b = wgu[fc % 3]
                for (c0, n) in tiles:
                    pa, pu, s = pb[it % 2], pb[2 + it % 2], st[it % 2]
                    it += 1
                    for c in range(NCH):
                        P.mm(pa[:, 0:n], b[:, c, 0:128], xn[:, c, c0:c0 + n], start=(c == 0), stop=(c == NCH - 1))
                    for c in range(NCH):
                        P.mm(pu[:, 0:n], b[:, c, 128:256], xn[:, c, c0:c0 + n], start=(c == 0), stop=(c == NCH - 1))
                    P.act(s[:, 0:n], pa[:, 0:n], AF.Silu)
                    P.tt("dve", g[:, fl, c0:c0 + n], pu[:, 0:n], s[:, 0:n], ALU.mult)
                if fc + 3 < NFC:
                    load_gu(fc + 3)
            wb = wd[gi % 2]
            last = (gi == len(FGROUPS) - 1)

            def down(dc, c0, n):
                py = pb[4 + cntd["ky"] % 2]
                cntd["ky"] += 1
                for fl in range(nf):
                    P.mm(py[:, 0:n], wb[:, fl, dc * 128:(dc + 1) * 128], g[:, fl, c0:c0 + n],
                         start=(fl == 0), stop=(fl == nf - 1))
                P.stt("dve", x[:, dc, c0:c0 + n], py[:, 0:n], 0.5, x[:, dc, c0:c0 + n], ALU.mult, ALU.add)

            if last and next_norm is not None:
                A2 = Arena(arena_ap, ARENA_BYTES)
                A2.off = off_wgu
                nb = norm_bufs(A2)
                prevt = None
                for (c0, n) in tiles:
                    for dc in range(NCH):
                        down(dc, c0, n)
                    if prevt is not None:
                        norm_tile(next_norm[0], next_norm[1], prevt[0], prevt[1], nb)
                    prevt = (c0, n)
                norm_tile(next_norm[0], next_norm[1], prevt[0], prevt[1], nb)
            else:
                for dc in range(NCH):
                    for (c0, n) in tiles:
                        down(dc, c0, n)
            if gi + 2 < len(FGROUPS):
                load_d(gi + 2)

    def transpose_out(src_cols, ncols, stg, col_off, ps):
        P.mm(ps[0:ncols, 0:128], src_cols, ident32)
        P.copy("act", stg[0:ncols, col_off:col_off + 128], ps[0:ncols, 0:128])

    def pool_part(seq, l):
        T = seq.T
        win_d = W["w_in"][l].rearrange("(c p) f -> p c f", p=128)
        A.reset()
        wp = [A.alloc([128, NCH, 128], BF16) for _ in range(2)]
        ups = [A.alloc([128, 16 + SEQ], F32) for _ in range(2)]
        sa = A.alloc([128, 16 + SEQ], F32)
        sbf = A.alloc([128, 16 + SEQ], F32)
        dds = [A.alloc([128, SEQ], BF16) for _ in range(2)]
        stg = A.alloc([16, 256], F32)
        tmp16 = A.alloc([128, 16], F32)
        for cc in range(2):
            P.dma("pool", wp[cc], win_d[:, :, cc * 128:(cc + 1) * 128])
        if seq.kind == "sample":
            prev = A.alloc([16, 256], F32)
            P.dma("sp", prev[0:15, :], spool[l])
        kk = 0
        for cc in range(2):
            up = ups[cc]
            if seq.kind == "prompt":
                P.memset("dve", up[:, 0:16], 0.0)
            else:
                P.memset("dve", up[:, 0:1], 0.0)
                P.mm(pb[7][:, 0:15], prev[0:15, cc * 128:(cc + 1) * 128], ident32[0:15, 0:15])
                P.copy("act", up[:, 1:16], pb[7][:, 0:15])
            for (c0, n) in seq.ttiles:
                ps = pb[kk % 2]
                kk += 1
                for c in range(NCH):
                    P.mm(ps[:, 0:n], wp[cc][:, c, :], xn[:, c, c0:c0 + n], start=(c == 0), stop=(c == NCH - 1))
                t0 = c0 - seq.col0
                P.copy("act", up[:, 16 + t0:16 + t0 + n], ps[:, 0:n])
            transpose_out(up[:, T + 1:T + 16], 15, stg, cc * 128, pb[7][:, 256:384])
        dst = npool_p[l, seq.b] if seq.kind == "prompt" else npool_s[l]
        P.dma("sp", dst, stg[0:15, :])
        for cc in range(2):
            up, dd = ups[cc], dds[cc]
            L = 16 + T
            P.tt("dve", sa[:, 2:L], up[:, 2:L], up[:, 1:L - 1], ALU.add)
            P.tt("dve", sbf[:, 4:L], sa[:, 4:L], sa[:, 2:L - 2], ALU.add)
            if cc == 0:
                lo, hi, wl, wh = sa, sbf, 2.0, 4.0
            else:
                P.tt("dve", sa[:, 8:L], sbf[:, 8:L], sbf[:, 4:L - 4], ALU.add)
                P.tt("dve", sbf[:, 16:L], sa[:, 16:L], sa[:, 8:L - 8], ALU.add)
                lo, hi, wl, wh = sa, sbf, 8.0, 16.0
            for (pr, srcb, w) in ((slice(0, 64), lo, wl), (slice(64, 128), hi, wh)):
                P.stt("dve", dd[pr, 0:T], srcb[pr, 16:16 + T], 1.0 / w, up[pr, 16:16 + T], ALU.mult, ALU.subtract)
                if seq.kind == "prompt":
                    P.tt("dve", tmp16[pr, :], srcb[pr, 16:32], invcnt[pr, cc, :], ALU.mult)
                    P.tt("dve", dd[pr, 0:16], tmp16[pr, :], up[pr, 16:32], ALU.subtract)
            for (c0, n) in seq.ttiles:
                ps = pb[2 + kk % 2]
                kk += 1
                t0 = c0 - seq.col0
                P.mm(ps[:, 0:n], pwb[:, l, cc, :], dd[:, t0:t0 + n])
                P.act(cat[:, cc, c0:c0 + n], ps[:, 0:n], AF.Copy, scale=gvec("pool_scale", l, cc))

    def conv_part(seq, l, hook=None):
        T = seq.T
        win_d = W["w_in"][l].rearrange("(c p) f -> p c f", p=128)
        A.reset()
        wa = [A.alloc([128, NCH, 128], BF16) for _ in range(2)]
        wg = [A.alloc([128, NCH, 128], BF16) for _ in range(2)]
        z = A.alloc([128, 2, 32 + SEQ], F32)
        acc = A.alloc([128, 2, SEQ], F32)
        sg = [A.alloc([128, 512], F32) for _ in range(2)]
        sq = A.alloc([128, 2, 512], F32)
        mu = A.alloc([128, 512], F32)
        m2 = A.alloc([128, 512], F32)
        sd = A.alloc([128, 512], F32)
        rs = A.alloc([128, 512], F32)
        t1 = [A.alloc([128, 512], F32) for _ in range(2)]
        stg = A.alloc([32, 256], F32)
        for cc in range(2):
            P.dma("pool", wa[cc], win_d[:, :, 256 + cc * 128:256 + (cc + 1) * 128])
            P.dma("pool", wg[cc], win_d[:, :, 512 + cc * 128:512 + (cc + 1) * 128])
        if seq.kind == "sample":
            prev = A.alloc([32, 256], F32)
            P.dma("sp", prev[0:30, :], sconv[l])
        kk = 0
        for cc in range(2):
            if seq.kind == "prompt":
                P.memset("dve", z[:, cc, 0:32], 0.0)
            else:
                P.memset("dve", z[:, cc, 0:2], 0.0)
                P.mm(pb[7][:, 0:30], prev[0:30, cc * 128:(cc + 1) * 128], ident32[0:30, 0:30])
                P.copy("act", z[:, cc, 2:32], pb[7][:, 0:30])
            for (c0, n) in seq.ttiles:
                pa, pg, s = pb[kk % 2], pb[2 + kk % 2], sg[kk % 2]
                kk += 1
                for c in range(NCH):
                    P.mm(pa[:, 0:n], wa[cc][:, c, :], xn[:, c, c0:c0 + n], start=(c == 0), stop=(c == NCH - 1))
                for c in range(NCH):
                    P.mm(pg[:, 0:n], wg[cc][:, c, :], xn[:, c, c0:c0 + n], start=(c == 0), stop=(c == NCH - 1))
                t0 = c0 - seq.col0
                P.act(s[:, 0:n], pg[:, 0:n], AF.Sigmoid)
                P.tt("dve", z[:, cc, 32 + t0:32 + t0 + n], pa[:, 0:n], s[:, 0:n], ALU.mult)
            transpose_out(z[:, cc, T + 2:T + 32], 30, stg, cc * 128, pb[7][:, 256:384])
        def tap(j, cc):
            k = (l * 31 + j) * 2 + cc
            return cwT[:, k:k + 1]
        for cc in range(2):
            P.ts("dve", acc[:, cc, 0:T], z[:, cc, 2:2 + T], tap(0, cc), gvec("conv_b", l, cc), ALU.mult, ALU.add)
        for j in range(1, 31):
            for cc in range(2):
                P.stt("dve", acc[:, cc, 0:T], z[:, cc, 2 + j:2 + j + T], tap(j, cc), acc[:, cc, 0:T],
                      ALU.mult, ALU.add)
                if hook is not None:
                    hook()
        dst = nconv_p[l, seq.b] if seq.kind == "prompt" else nconv_s[l]
        P.dma("sp", dst, stg[0:30, :])
        kk = 0
        for (c0, n) in seq.ttiles:
            t0 = c0 - seq.col0
            a2 = acc[:, :, t0:t0 + n]
            P.act(sq[:, :, 0:n], a2, AF.Square)
            for cc in range(2):
                P.mm(pb[4][:, 0:n], ones32, acc[:, cc, t0:t0 + n], start=(cc == 0), stop=(cc == 1))
            for cc in range(2):
                P.mm(pb[5][:, 0:n], ones32, sq[:, cc, 0:n], start=(cc == 0), stop=(cc == 1))
            P.act(mu[:, 0:n], pb[4][:, 0:n], AF.Copy, scale=1.0 / 256)
            P.tt("dve", m2[:, 0:n], mu[:, 0:n], mu[:, 0:n], ALU.mult)
            P.stt("dve", m2[:, 0:n], pb[5][:, 0:n], 1.0 / 256, m2[:, 0:n], ALU.mult, ALU.subtract)
            P.ts("dve", m2[:, 0:n], m2[:, 0:n], 0.0, None, ALU.max)
            P.act(sd[:, 0:n], m2[:, 0:n], AF.Sqrt, bias=epsT, scale=1.0)
            P.op("dve", lambda e, o=rs[:, 0:n], i=sd[:, 0:n]: e.reciprocal(o, i),
                 reads=[sd[:, 0:n]], writes=[rs[:, 0:n]])
            for cc in range(2):
                t = t1[kk % 2]
                kk += 1
                P.tt("dve", t[:, 0:n], acc[:, cc, t0:t0 + n], mu[:, 0:n], ALU.subtract)
                P.tt("dve", t[:, 0:n], t[:, 0:n], rs[:, 0:n], ALU.mult)
                P.act(cat[:, 2 + cc, c0:c0 + n], t[:, 0:n], AF.Silu,
                      bias=gvec("conv_ln_b", l, cc), scale=gvec("conv_ln_g", l, cc))

    def att_prologue(seq, l):
        T = seq.T
        sample = seq.kind == "sample"
        win_d = W["w_in"][l].rearrange("(c p) f -> p c f", p=128)
        NN = len(seq.ktiles)
        NP = PAST // 128 if sample else 0
        NT = NP + NN
        P.dma("pool", wf, win_d[:, :, 2304:2312])
        if sample:
            cl = clf[l].rearrange("(j p) h -> p j h", p=128)
            for q in range(4):
                P.dma("sp", lf[:, q * 8:(q + 1) * 8, :], cl[:, q * 8:(q + 1) * 8, :])
        for ti, (t0, n) in enumerate(seq.ktiles):
            ps = pb[6][0:n, ti * 8:(ti + 1) * 8]
            cs = seq.col0 + t0
            for c in range(NCH):
                P.mm(ps, xn[:, c, cs:cs + n], wf[:, c, :], start=(c == 0), stop=(c == NCH - 1))
            P.tt("dve", ua[0:n, ti, :], ps, fbT[0:n, l, :], ALU.add)
            yield
        nrow = seq.ktiles[0][1]
        uv = ua[0:nrow, 0:NN, :]
        P.act(uv, uv, AF.Exp, scale=-1.0)
        P.act(uv, uv, AF.Ln, bias=1.0)
        P.ts("dve", lf[0:nrow, NP:NP + NN, :], uv, -1.0, None, ALU.mult)
        if sample:
            P.dma("sp", nf_s[l], lf[0:nrow, NP, :])
        else:
            P.dma("sp", nf_p[l, seq.b].rearrange("(j p) h -> p j h", p=128), lf[:, 0:NN, :])
        yield
        P.memset("dve", Sp[0], 0.0)
        P.memset("dve", rall[:, 0, :], 0.0)
        for j in range(NT):
            n = 128 if j < NP else seq.ktiles[j - NP][1]
            ps = pb[6][0:n, 256 + (j % 16) * 8:256 + (j % 16) * 8 + 8]
            s_cur, s_nxt = Sp[j % 2], Sp[(j + 1) % 2]
            P.mm(ps, tri32[0:n, 0:n], lf[0:n, j, :], start=True, stop=(j == 0))
            if j > 0:
                P.mm(ps, ones32[:, 0:n], s_cur, start=False, stop=True)
                pr = pb[6][:, 384 + (j % 16) * 8:384 + (j % 16) * 8 + 8]
                P.mm(pr, ones32, s_cur)
                P.copy("act", rall[:, j, :], pr)
            P.copy("act", ctok[0:n, :, j], ps)
            if j + 1 < NT:
                P.tt("dve", s_nxt, s_cur, lf[:, j, :], ALU.add)
            yield
        if not sample:
            for I in range(4):
                P.tt("dve", alpha[:, 4 * I:4 * I + 4, :], rall[:, 4 * I:4 * I + 4, :],
                     rall[:, 4 * I:4 * I + 1, :].broadcast_to([128, 4, 8]), ALU.subtract)
            P.act(alpha, alpha, AF.Exp)
        yield

    def att_part(seq, l):
        T = seq.T
        sample = seq.kind == "sample"
        win_d = W["w_in"][l].rearrange("(c p) f -> p c f", p=128)
        NN = len(seq.ktiles)
        NP = PAST // 128 if sample else 0
        NT = NP + NN
        A.reset()
        biasb = [A.alloc([128, 2, 33], F32) for _ in range(2)]
        wqkv = [A.alloc([128, NCH, 384], BF16) for _ in range(2)]
        QT = A.alloc([128, SEQ], BF16)
        KT = A.alloc([128, SEQ], BF16)
        VA = A.alloc([128, 16, 2, 128], BF16)
        sqb = [A.alloc([128, 256], F32) for _ in range(2)]
        ssb = [A.alloc([128, 4], F32) for _ in range(2)]
        sdb = [A.alloc([128, 4], F32) for _ in range(2)]
        qkn = [A.alloc([128, 4, 64], F32) for _ in range(3)]
        vst = [A.alloc([128, 128], F32) for _ in range(3)]
        PTW = 128 if sample else 512
        PT = [A.alloc([128, 2, PTW], BF16) for _ in range(3)]
        rec = [A.alloc([128, 2, 128], F32) for _ in range(2)]
        if not sample:
            toff = [A.alloc([128, 2, 128], F32) for _ in range(2)]
        if sample:
            kraw = [A.alloc([128, 4, 128], F32) for _ in range(3)]
            vraw32 = [A.alloc([128, 4, 128], F32) for _ in range(3)]
            KTg = [A.alloc([128, 4, 128], BF16) for _ in range(2)]
            ebb = [A.alloc([128, 2, 33], F32) for _ in range(2)]
            vraw = [A.alloc([128, 4, 2, 128], BF16) for _ in range(3)]
            for v in vraw:
                P.memset("dve", v[:, :, :, 64:128], 1.0)
        P.memset("dve", VA[:, :, :, 64:128], 1.0)

        def load_qkv(p):
            b = wqkv[p % 2]
            for k in range(3):
                P.dma("pool", b[:, :, k * 128:(k + 1) * 128],
                      win_d[:, :, 768 + 512 * k + p * 128:768 + 512 * k + (p + 1) * 128])

        load_qkv(0)
        load_qkv(1)
        if ATT_LEVEL < 3:
            return
        qtiles = [(NP + i, t0, n) for i, (t0, n) in enumerate(seq.ktiles)]
        cnt = {"s": 0, "o": 0, "pt": 0, "kr": 0, "b": 0, "q": 0, "r": 0, "tp": 0}
        dstk = nk_s[l] if sample else nk_p[l, seq.b]
        dstv = nv_s[l] if sample else nv_p[l, seq.b]
        QKB = [pb[0], pb[2], pb[3]]
        TRB = [pb[1], pb[4]]
        for p in range(4):
            wb = wqkv[p % 2]
            def projA(ti, t0, n):
                cs = seq.col0 + t0
                q_ = cnt["q"]
                cnt["q"] += 1
                ps = QKB[q_ % 3]
                sq_, ss_, sd_ = sqb[q_ % 2], ssb[q_ % 2], sdb[q_ % 2]
                qk, vs = qkn[q_ % 3], vst[q_ % 3]
                for c in range(NCH):
                    P.mm(ps[0:n, 0:384], xn[:, c, cs:cs + n], wb[:, c, :], start=(c == 0), stop=(c == NCH - 1))
                P.act(sq_[0:n, :], ps[0:n, 0:256], AF.Square)
                P.op("dve", lambda e, o=ss_[0:n, :], i=sq_[0:n, :].rearrange("p (a b) -> p a b", a=4):
                     e.tensor_reduce(o, i, AX.X, ALU.add), reads=[sq_[0:n, :]], writes=[ss_[0:n, :]])
                P.act(sd_[0:n, :], ss_[0:n, :], AF.Sqrt, bias=epsT[0:n, :], scale=1.0 / 64)
                P.op("dve", lambda e, o=ss_[0:n, :], i=sd_[0:n, :]: e.reciprocal(o, i),
                     reads=[sd_[0:n, :]], writes=[ss_[0:n, :]])
                for k4 in range(4):
                    P.stt("dve", qk[0:n, k4, :], ps[0:n, k4 * 64:(k4 + 1) * 64], ss_[0:n, k4:k4 + 1],
                          gqk[0:n, l, k4, :], ALU.mult, ALU.mult)
                P.copy("act", vs[0:n, :], ps[0:n, 256:384])
                P.copy("dve", VA[0:n, ti, :, 0:64], ps[0:n, 256:384].rearrange("p (a b) -> p a b", a=2))
                P.dma("sp", dstk[t0:t0 + n, p * 128:(p + 1) * 128], qk[0:n, 2:4, :].rearrange("p a b -> p (a b)"))
                P.dma("sp", dstv[t0:t0 + n, p * 128:(p + 1) * 128], vs[0:n, :])
                return (qk, t0, n)

            def projB(st):
                qk, t0, n = st
                pt_ = TRB[cnt["tp"] % 2]
                cnt["tp"] += 1
                P.mm(pt_[:, 0:n], qk[0:n, 0:2, :].rearrange("p a b -> p (a b)"), ident32[0:n, 0:n])
                P.mm(pt_[:, 128:128 + n], qk[0:n, 2:4, :].rearrange("p a b -> p (a b)"), ident32[0:n, 0:n])
                P.copy("act", QT[:, t0:t0 + n], pt_[:, 0:n])
                P.copy("act", KT[:, t0:t0 + n], pt_[:, 128:128 + n])

            pend = []
            for ti, (t0, n) in enumerate(seq.ktiles):
                pend.append(projA(ti, t0, n))
                if len(pend) > 2:
                    projB(pend.pop(0))
            while pend:
                projB(pend.pop(0))
            if p + 2 < 4:
                load_qkv(p + 2)
            if ATT_LEVEL < 4:
                continue
            items = []
            if sample:
                for (gi, t0, nq) in qtiles:
                    for j in range(gi + 1):
                        items.append(("diag", gi, t0, nq, j, 0, gi))
            else:
                for I in range(4):
                    for j in range(4 * I):
                        items.append(("off", 4 * I, 512 * I, 512, j, 0, 4 * I - 1))
                    for i in range(4 * I, 4 * I + 4):
                        for j in range(4 * I, i + 1):
                            items.append(("diag", i, 128 * i, 128, j, 4 * I, i))
            state = {"kgrp": None, "bgi": None}
            groups = {}

            def prep_dma(g):
                kb, v32 = kraw[g % 3], vraw32[g % 3]
                j0 = g * 4
                P.dma("sp", kb, ck[l, j0 * 128:(j0 + 4) * 128, p * 128:(p + 1) * 128]
                      .rearrange("(j p) c -> p j c", p=128))
                P.dma("sp", v32, cv[l, j0 * 128:(j0 + 4) * 128, p * 128:(p + 1) * 128]
                      .rearrange("(j p) c -> p j c", p=128))

            def prep_tr(g):
                kb, v32, vb, ktg = kraw[g % 3], vraw32[g % 3], vraw[g % 3], KTg[g % 2]
                for q4 in range(4):
                    P.mm(pb[1][:, q4 * 128:(q4 + 1) * 128], kb[:, q4, :], ident32)
                P.copy("dve", ktg, pb[1].rearrange("p (a b) -> p a b", a=4))
                eb = state["eb"]
                for q4 in range(4):
                    for hh in range(2):
                        sc = eb[:, hh, 4 * g + q4:4 * g + q4 + 1]
                        P.act(vb[:, q4, hh, 0:64], v32[:, q4, hh * 64:(hh + 1) * 64], AF.Copy, scale=sc)
                        P.ts("dve", vb[:, q4, hh, 64:128], ones32[:, 0:64], sc, None, ALU.mult)
                groups[g] = (ktg, vb)

            def scores(it):
                kind, gi, t0, nq, j, jf, jl_ = it
                if state["bgi"] != gi:
                    bb = biasb[cnt["b"] % 2]
                    cnt["b"] += 1
                    for hh in range(2):
                        P.ts("dve", bb[:, hh, 0:gi + 1], ctok[:, 2 * p + hh, 0:gi + 1], -1.0,
                             rall[:, gi, 2 * p + hh:2 * p + hh + 1], ALU.mult, ALU.add)
                    state["bb"] = bb
                    state["bgi"] = gi
                if j == jf:
                    if kind == "off":
                        state["po"] = None
                    else:
                        state["po"] = pb[6 + cnt["o"] % 2].rearrange("p (a b) -> p a b", a=4)
                        cnt["o"] += 1
                bb, po = state["bb"], state["po"]
                psS = [pb[2 + cnt["s"] % 2], pb[4 + cnt["s"] % 2]]
                cnt["s"] += 1
                ptile = PT[cnt["pt"] % 3]
                cnt["pt"] += 1
                diag = (kind == "diag" and j == gi)
                if j < NP:
                    g = j // 4
                    if j % 4 == 0:
                        ng = NP // 4
                        if g == 0:
                            prep_dma(0)
                            prep_dma(1)
                            prep_tr(0)
                        if g + 2 < ng:
                            prep_dma(g + 2)
                        if g + 1 < ng:
                            prep_tr(g + 1)
                    ktg, vb = groups[g]
                    nk = 128
                    kT = [ktg[hh * 64:(hh + 1) * 64, j % 4, :] for hh in range(2)]
                    vv = [vb[:, j % 4, hh, :] for hh in range(2)]
                else:
                    jj = j - NP
                    k0, nk = seq.ktiles[jj]
                    kT = [KT[hh * 64:(hh + 1) * 64, k0:k0 + nk] for hh in range(2)]
                    vv = [VA[0:nk, jj, hh, :] for hh in range(2)]
                for hh in range(2):
                    P.mm(psS[hh][0:nk, 0:nq], kT[hh], QT[hh * 64:(hh + 1) * 64, t0:t0 + nq],
                         start=True, stop=not diag)
                if diag:
                    for hh in range(2):
                        P.mm(psS[hh][0:nk, 0:nq], identb[:, 0:nk], maskb[:, 0:nq], start=False, stop=True)
                return dict(it=it, psS=psS, ptile=ptile, nk=nk, vv=vv, bb=bb, po=po)

            def exps(d):
                kind, gi, t0, nq, j, jf, jl_ = d["it"]
                nk = d["nk"]
                for hh in range(2):
                    P.act(d["ptile"][0:nk, hh, 0:nq], d["psS"][hh][0:nk, 0:nq], AF.Exp,
                          bias=d["bb"][0:nk, hh, j:j + 1], scale=0.125)

            def pv(d):
                kind, gi, t0, nq, j, jf, jl_ = d["it"]
                nk, po = d["nk"], d["po"]
                if kind == "off":
                    for hh in range(2):
                        P.mm(pb[hh][:, 0:nq], d["vv"][hh], d["ptile"][0:nk, hh, 0:nq], start=(j == jf), stop=(j == jl_))
                    return
                for hh in range(2):
                    P.mm(po[:, hh, 0:nq], d["vv"][hh], d["ptile"][0:nk, hh, 0:nq], start=(j == jf and hh == 0),
                         stop=(j == jl_), skip=True)
                if j == jl_:
                    rc = rec[cnt["r"] % 2]
                    cnt["r"] += 1
                    src = po
                    if (not sample) and gi >= 4:
                        tf = toff[cnt["r"] % 2]
                        sub = (gi % 4) * 128
                        for hh in range(2):
                            P.ts("dve", tf[:, hh, :], pb[hh][:, sub:sub + 128],
                                 alpha[:, gi, 2 * p + hh:2 * p + hh + 1], None, ALU.mult)
                        P.tt("dve", tf[:, :, 0:nq], tf[:, :, 0:nq], po[:, 0:2, 0:nq], ALU.add)
                        src = tf
                    P.op("dve", lambda e, o=rc[64:128, :, 0:nq], i=src[64:128, 0:2, 0:nq]: e.reciprocal(o, i),
                         reads=[src[64:128, 0:2, 0:nq]], writes=[rc[64:128, :, 0:nq]])
                    P.copy("dve", rc[0:64, :, 0:nq], rc[64:128, :, 0:nq])
                    cs = seq.col0 + t0
                    P.tt("dve", cat[0:64, 4 + p, cs:cs + nq], src[0:64, 0, 0:nq], rc[0:64, 0, 0:nq], ALU.mult)
                    P.tt("dve", rc[0:64, 0, 0:nq], src[0:64, 1, 0:nq], rc[0:64, 1, 0:nq], ALU.mult)
                    P.copy("dve", cat[64:128, 4 + p, cs:cs + nq], rc[0:64, 0, 0:nq])

            if sample:
                gi, t0, nq = qtiles[0]
                bb = biasb[cnt["b"] % 2]
                cnt["b"] += 1
                ebt = ebb[p % 2]
                for hh in range(2):
                    P.ts("dve", bb[:, hh, 0:gi + 1], ctok[:, 2 * p + hh, 0:gi + 1], -1.0,
                         rall[:, gi, 2 * p + hh:2 * p + hh + 1], ALU.mult, ALU.add)
                P.act(ebt[:, :, 0:NP], bb[:, :, 0:NP], AF.Exp)
                state["eb"] = ebt
                state["bb"] = bb
                state["bgi"] = gi
                po = pb[6 + cnt["o"] % 2].rearrange("p (a b) -> p a b", a=4)
                cnt["o"] += 1
                state["po"] = po
                ng = NP // 4
                prep_dma(0)
                prep_dma(1)
                prep_tr(0)

                def g_scores(g):
                    if g + 2 < ng:
                        prep_dma(g + 2)
                    if g + 1 < ng:
                        prep_tr(g + 1)
                    ktg, vb = groups[g]
                    psS = [pb[2 + cnt["s"] % 2], pb[4 + cnt["s"] % 2]]
                    cnt["s"] += 1
                    ptile = PT[cnt["pt"] % 3]
                    cnt["pt"] += 1
                    for q4 in range(4):
                        for hh in range(2):
                            P.mm(psS[hh][:, q4 * 32:q4 * 32 + nq], ktg[hh * 64:(hh + 1) * 64, q4, :],
                                 QT[hh * 64:(hh + 1) * 64, t0:t0 + nq], start=True, stop=True)
                    return (g, psS, ptile, vb)

                def g_exps(d):
                    g, psS, ptile, vb = d
                    for hh in range(2):
                        P.act(ptile[:, hh, 0:128], psS[hh][:, 0:128], AF.Exp, scale=0.125)

                def g_pv(d):
                    g, psS, ptile, vb = d
                    for q4 in range(4):
                        for hh in range(2):
                            P.mm(po[:, hh, 0:nq], vb[:, q4, hh, :], ptile[:, hh, q4 * 32:q4 * 32 + nq],
                                 start=(g == 0 and q4 == 0 and hh == 0), stop=False, skip=True)

                prevg = None
                for g in range(ng):
                    d = g_scores(g)
                    if prevg is not None:
                        g_pv(prevg)
                    g_exps(d)
                    prevg = d
                it = ("diag", gi, t0, nq, gi, 0, gi)
                dd = scores(it)
                g_pv(prevg)
                exps(dd)
                pv(dd)
            else:
                prevd = None
                for it in items:
                    d = scores(it)
                    if prevd is not None:
                        pv(prevd)
                    exps(d)
                    prevd = d
                pv(prevd)

    def wout_part(seqs, l, next_norm=None):
        A.reset()
        wo = A.alloc([128, NCH, D], BF16)
        P.dma("pool", wo, W["w_out"][l].rearrange("(c p) d -> p c d", p=128))
        nb = norm_bufs(A) if next_norm is not None else None
        tiles = [tt for s in seqs for tt in s.ttiles]
        k = 0
        prevt = None
        for (c0, n) in tiles:
            for dc in range(NCH):
                ps = pb[k % 2]
                k += 1
                for kc in range(NCH):
                    P.mm(ps[:, 0:n], wo[:, kc, dc * 128:(dc + 1) * 128], cat[:, kc, c0:c0 + n],
                         start=(kc == 0), stop=(kc == NCH - 1))
                P.tt("dve", x[:, dc, c0:c0 + n], ps[:, 0:n], x[:, dc, c0:c0 + n], ALU.add)
            if next_norm is not None:
                if prevt is not None:
                    norm_tile(next_norm[0], next_norm[1], prevt[0], prevt[1], nb)
                prevt = (c0, n)
        if next_norm is not None:
            norm_tile(next_norm[0], next_norm[1], prevt[0], prevt[1], nb)

    ORDER = ["load", "ffn1", "norm", "pool", "conv", "att", "wout", "ffn2"]
    def upto(name, l):
        if stage is None:
            return True
        sl, sn = stage
        if l < sl:
            return True
        if l > sl:
            return False
        return ORDER.index(name) <= ORDER.index(sn)
    passes = [[Seq("prompt", 0, SEQ, 0)], [Seq("prompt", 0, SEQ, 1), Seq("sample", SEQ, DEC, 0)]][:npass]
    for seqs in passes:
        A.reset()
        for s in seqs:
            load_x(s)
        for l in range(2):
            full = stage is None
            if upto("ffn1", l):
                ffn(seqs, l, "ffn1", prenormed=(full and l > 0), next_norm=("mix_norm", l) if full else None)
            if upto("norm", l) and not full:
                rmsnorm(seqs, "mix_norm", l)
            for s in seqs:
                gen = att_prologue(s, l) if upto("att", l) else iter(())
                if s.kind == "sample":
                    for _ in gen:
                        pass
                if upto("pool", l):
                    pool_part(s, l)
                if upto("conv", l):
                    conv_part(s, l, hook=lambda g=gen: next(g, None))
                for _ in gen:
                    pass
                if upto("att", l):
                    att_part(s, l)
            if upto("wout", l):
                wout_part(seqs, l, next_norm=("ffn2_norm", l) if full else None)
            if upto("ffn2", l):
                ffn(seqs, l, "ffn2", prenormed=full,
                    next_norm=("ffn1_norm", l + 1) if (full and l + 1 < 2) else None)
        A.reset()
        for s in seqs:
            store_y(s)
    P.emit()
    return nc


_NC_CACHE = {}


def _consts():
    ident = np.eye(128, dtype=np.float32)
    tri = np.triu(np.ones((128, 128), dtype=np.float32))
    kk = np.arange(128)[:, None]
    qq = np.arange(128)[None, :]
    mask = np.where(kk > qq, -30000.0, 0.0).astype(np.float32)
    inv = np.zeros((128, 2, 16), dtype=np.float32)
    wins = (2, 4, 8, 16)
    for cc in range(2):
        for p in range(128):
            w = wins[2 * cc + p // 64]
            for t in range(16):
                inv[p, cc, t] = 1.0 / min(t + 1, w)
    return ident, tri, mask, inv


def kernel(**inputs):
    if "nc" not in _NC_CACHE:
        _NC_CACHE["nc"] = build_program()
    nc = _NC_CACHE["nc"]
    f = lambda a: np.ascontiguousarray(np.asarray(a, dtype=np.float32))
    ident, tri, mask, inv = _consts()
    wnames = ["ffn1_norm", "ffn1_w_gu", "ffn1_w_down", "mix_norm", "w_in", "w_out", "pool_w", "pool_scale",
              "conv_w", "conv_b", "conv_ln_g", "conv_ln_b", "q_norm", "k_norm", "forget_b",
              "ffn2_norm", "ffn2_w_gu", "ffn2_w_down"]
    shared = {k: f(inputs[k]) for k in wnames}
    shared.update(c_ident=ident, c_tri=tri, c_mask=mask, c_invcnt=inv)
    xpr, xsm = f(inputs["x_prompt"]), f(inputs["x_sample"])
    sp_, sc_ = f(inputs["state_pool"]), f(inputs["state_conv"])
    ck_, cv_, cl_ = f(inputs["cache_k"]), f(inputs["cache_v"]), f(inputs["cache_logf"])
    in_maps = []
    for c in range(8):
        m = dict(shared)
        m["xp"] = np.ascontiguousarray(xpr[2 * c:2 * c + 2])
        m["xs"] = np.ascontiguousarray(xsm[c])
        m["spool"] = np.ascontiguousarray(sp_[:, c])
        m["sconv"] = np.ascontiguousarray(sc_[:, c])
        m["ck"] = np.ascontiguousarray(ck_[:, c].reshape(2, PAST, 512))
        m["cv"] = np.ascontiguousarray(cv_[:, c].reshape(2, PAST, 512))
        m["clf"] = np.ascontiguousarray(cl_[:, c])
        in_maps.append(m)
    res = run_bass_kernel_spmd(nc, in_maps, core_ids=list(range(8)))
    R = res.results
    cat0 = lambda k: np.concatenate([r[k] for r in R], axis=0)
    cat1 = lambda k: np.concatenate([r[k] for r in R], axis=1)
    st1 = lambda k: np.stack([r[k] for r in R], axis=1)
    y_prompt = cat0("y_p")
    y_sample = np.stack([r["y_s"] for r in R], axis=0)
    outs = (
        y_prompt, y_sample,
        cat1("npool_p"), st1("npool_s"),
        cat1("nconv_p"), st1("nconv_s"),
        cat1("nk_p").reshape(2, 16, SEQ, 8, 64), cat1("nv_p").reshape(2, 16, SEQ, 8, 64), cat1("nf_p"),
        st1("nk_s").reshape(2, 8, DEC, 8, 64), st1("nv_s").reshape(2, 8, DEC, 8, 64), st1("nf_s"),
    )
    return tuple(np.ascontiguousarray(o, dtype=np.float32) for o in outs)
```

```python
import numpy as np
import concourse.bass as bass
import concourse.mybir as mybir
from concourse.bass_utils import run_bass_kernel_spmd

F32 = mybir.dt.float32
BF16 = mybir.dt.bfloat16
AF = mybir.ActivationFunctionType
ALU = mybir.AluOpType
AX = mybir.AxisListType

_DT_SIZE = {F32: 4, BF16: 2}


class Op:
    __slots__ = ("eng", "fn", "deps", "is_dma", "signal", "sem", "val", "idx", "seq")

    def __init__(self, eng, fn, is_dma):
        self.eng = eng
        self.fn = fn
        self.deps = []
        self.is_dma = is_dma
        self.signal = False
        self.sem = None
        self.val = 0


class Prog:
    ENGS = ("pe", "act", "dve", "pool", "sp")

    def __init__(self, nc, n_dma_sems=14):
        self.nc = nc
        self.ops = {e: [] for e in self.ENGS}
        self.acc = {}
        self.n_dma_sems = n_dma_sems
        self.tracked = set()
        self.nbuf = 0
        self.waited = {e: {} for e in self.ENGS}
        self.psum = set()

    def sb(self, name, shape, dt):
        t = self.nc.alloc_sbuf_tensor(name, list(shape), dt)
        self.tracked.add(name)
        return t.ap()

    def ps(self, name, shape, dt=F32):
        t = self.nc.alloc_psum_tensor(name, list(shape), dt)
        self.tracked.add(name)
        self.psum.add(name)
        return t.ap()

    def _regions(self, ap):
        name = ap.tensor.name
        if name not in self.tracked:
            return ()
        pat = ap.ap
        off = ap.offset
        pstride = pat[0][0]
        if pstride <= 0:
            pstride = 1 << 40
        p0 = off // pstride
        f0 = off % pstride
        p1 = p0 + pat[0][1]
        sz = _DT_SIZE.get(ap.dtype, 4)
        free = list(pat[1:])
        outer = [(0, 1)]
        if len(free) >= 2 and free[0][1] <= 40 and free[0][0] > 0:
            outer = [(free[0][0], free[0][1])]
            free = free[1:]
        ext = 0
        for st, cnt in free:
            ext += abs(st) * (cnt - 1)
        res = []
        ost, ocnt = outer[0]
        for i in range(ocnt):
            a = f0 + i * ost
            res.append((name, p0, p1, a * sz, (a + ext + 1) * sz))
        return res

    def op(self, eng, fn, reads=(), writes=(), dma=False):
        o = Op(eng, fn, dma)
        deps = set()
        rregs = []
        wregs = []
        for ap in reads:
            rregs.extend(self._regions(ap))
        for ap in writes:
            wregs.extend(self._regions(ap))
        for (name, p0, p1, f0, f1) in rregs:
            lst = self.acc.get(name)
            if not lst:
                continue
            if name in self.psum:
                for (q0, q1, g0, g1, po, w) in lst:
                    if po.eng != eng or (w and q0 < p1 and p0 < q1 and g0 < f1 and f0 < g1):
                        deps.add(po)
                continue
            for (q0, q1, g0, g1, po, w) in lst:
                if w and q0 < p1 and p0 < q1 and g0 < f1 and f0 < g1:
                    deps.add(po)
        for (name, p0, p1, f0, f1) in wregs:
            lst = self.acc.get(name)
            if not lst:
                continue
            keep = []
            isps = name in self.psum
            for ent in lst:
                (q0, q1, g0, g1, po, w) = ent
                if isps and po.eng != eng:
                    deps.add(po)
                if q0 < p1 and p0 < q1 and g0 < f1 and f0 < g1:
                    deps.add(po)
                    if q0 >= p0 and q1 <= p1 and g0 >= f0 and g1 <= f1:
                        continue
                keep.append(ent)
            self.acc[name] = keep
        for (name, p0, p1, f0, f1) in rregs:
            self.acc.setdefault(name, []).append((p0, p1, f0, f1, o, False))
        for (name, p0, p1, f0, f1) in wregs:
            self.acc.setdefault(name, []).append((p0, p1, f0, f1, o, True))
        deps.discard(o)
        best = {}
        for d in deps:
            if d.is_dma:
                o.deps.append(d)
                continue
            if d.eng == "pe" and eng == "pe":
                continue
            b = best.get(d.eng)
            if b is None or d.seq > b.seq:
                best[d.eng] = d
        for d in best.values():
            w = self.waited[eng].get(d.eng, -1)
            if d.seq <= w:
                continue
            self.waited[eng][d.eng] = d.seq
            o.deps.append(d)
            d.signal = True
        o.seq = len(self.ops[eng])
        self.ops[eng].append(o)
        return o

    def mm(self, out, lhsT, rhs, start=True, stop=True, skip=False):
        if skip:
            return self.op("pe", lambda e: e.matmul(out, lhsT, rhs, start=start, stop=stop, skip_group_check=True),
                           reads=[lhsT, rhs], writes=[out])
        return self.op("pe", lambda e: e.matmul(out, lhsT, rhs, start=start, stop=stop),
                       reads=[lhsT, rhs], writes=[out])

    def act(self, out, in_, func, bias=None, scale=None, eng="act", extra_reads=()):
        kw = {}
        rd = [in_] + list(extra_reads)
        if bias is not None:
            kw["bias"] = bias
            if not isinstance(bias, (int, float)):
                rd.append(bias)
        if scale is not None:
            kw["scale"] = scale
            if not isinstance(scale, (int, float)):
                rd.append(scale)
        return self.op("act", lambda e: e.activation(out, in_, func, **kw), reads=rd, writes=[out])

    def tt(self, eng, out, in0, in1, op):
        return self.op(eng, lambda e: e.tensor_tensor(out, in0, in1, op), reads=[in0, in1], writes=[out])

    def ts(self, eng, out, in0, s1, s2, op0, op1=None):
        rd = [in0]
        if not isinstance(s1, (int, float)):
            rd.append(s1)
        if s2 is not None and not isinstance(s2, (int, float)):
            rd.append(s2)
        if op1 is None:
            return self.op(eng, lambda e: e.tensor_scalar(out, in0, s1, None, op0), reads=rd, writes=[out])
        return self.op(eng, lambda e: e.tensor_scalar(out, in0, s1, s2, op0, op1), reads=rd, writes=[out])

    def stt(self, eng, out, in0, scalar, in1, op0, op1):
        rd = [in0, in1]
        if not isinstance(scalar, (int, float)):
            rd.append(scalar)
        return self.op(eng, lambda e: e.scalar_tensor_tensor(out, in0, scalar, in1, op0, op1),
                       reads=rd, writes=[out])

    def copy(self, eng, out, in_):
        if eng == "act":
            return self.op("act", lambda e: e.copy(out, in_), reads=[in_], writes=[out])
        return self.op(eng, lambda e: e.tensor_copy(out, in_), reads=[in_], writes=[out])

    def memset(self, eng, out, val):
        return self.op(eng, lambda e: e.memset(out, val), writes=[out])

    def dma(self, q, out, in_):
        return self.op(q, lambda e: e.dma_start(out, in_), reads=[in_], writes=[out], dma=True)

    def emit(self):
        nc = self.nc
        esem = {e: nc.alloc_semaphore("S_" + e) for e in self.ENGS}
        dsem = {e: [nc.alloc_semaphore("D_%s_%d" % (e, i)) for i in range(self.n_dma_sems)]
                for e in ("pool", "sp", "act")}
        dcnt = {e: [0] * self.n_dma_sems for e in dsem}
        drr = {e: 0 for e in dsem}
        final_waits = []
        for e in self.ENGS:
            if self.ops[e]:
                self.ops[e][-1].signal = True
        for e in self.ENGS:
            cnt = 0
            for o in self.ops[e]:
                if o.is_dma:
                    k = drr[e]
                    drr[e] = (k + 1) % self.n_dma_sems
                    dcnt[e][k] += 16
                    o.sem = dsem[e][k]
                    o.val = dcnt[e][k]
                    o.signal = True
                elif o.signal:
                    cnt += 1
                    o.sem = esem[e]
                    o.val = cnt
        for e in dsem:
            for k in range(self.n_dma_sems):
                if dcnt[e][k]:
                    final_waits.append((dsem[e][k], dcnt[e][k]))
        ops = self.ops

        def run(engname, eobj, last=False):
            waited = {}
            for o in ops[engname]:
                need = {}
                for d in o.deps:
                    key = d.sem.num
                    if need.get(key, (None, 0))[1] < d.val:
                        need[key] = (d.sem, d.val)
                if o.is_dma and o.val > 16:
                    key = o.sem.num
                    if need.get(key, (None, 0))[1] < o.val - 16:
                        need[key] = (o.sem, o.val - 16)
                for key, (s, v) in need.items():
                    if waited.get(key, 0) >= v:
                        continue
                    eobj.wait_ge(s, v)
                    waited[key] = v
                ins = o.fn(eobj)
                if o.signal:
                    ins.then_inc(o.sem, 16 if o.is_dma else 1)
            if last:
                for s, v in final_waits:
                    if waited.get(s.num, 0) < v:
                        eobj.wait_ge(s, v)
                for e2 in self.ENGS:
                    if e2 == engname:
                        continue
                    sig = [o for o in ops[e2] if o.signal and not o.is_dma]
                    if sig:
                        eobj.wait_ge(esem[e2], sig[-1].val)

        with nc.Block() as block:
            @block.tensor
            def _(e):
                run("pe", e)

            @block.scalar
            def _(e):
                run("act", e)

            @block.vector
            def _(e):
                run("dve", e)

            @block.gpsimd
            def _(e):
                run("pool", e)

            @block.sync
            def _(e):
                run("sp", e, last=True)
D = 1024
NCH = 8
SEQ = 2048
DEC = 32
PAST = 4096
DFF = 2816
NFC = 22
DIN = 2312
EPS = 1e-6
TTMAX = SEQ + DEC
ARENA_BYTES = 66 * 1024
FGROUPS = [(0, 6), (6, 6), (12, 5), (17, 5)]
import os as _os
ATT_LEVEL = int(_os.environ.get('ATT_LEVEL', '4'))


class Arena:
    def __init__(self, ap_bf16, nbytes):
        self.ap = ap_bf16
        self.nbytes = nbytes
        self.off = 0

    def reset(self):
        self.off = 0

    def alloc(self, shape, dt):
        sz = _DT_SIZE[dt]
        n = 1
        for s in shape[1:]:
            n *= s
        nb = (n * sz + 63) // 64 * 64
        assert self.off + nb <= self.nbytes, ("arena overflow", self.off, nb, shape)
        v = self.ap[:, self.off // 2:(self.off + n * sz) // 2]
        self.off += nb
        if dt != BF16:
            v = v.bitcast(dt)
        if len(shape) == 3:
            v = v.rearrange("p (a b) -> p a b", a=shape[1])
        elif len(shape) == 4:
            v = v.rearrange("p (a b c) -> p a b c", a=shape[1], b=shape[2])
        if shape[0] < 128:
            v = v[0:shape[0]]
        return v


class Seq:
    def __init__(self, kind, col0, T, b):
        self.kind = kind
        self.col0 = col0
        self.T = T
        self.b = b
        self.ttiles = []
        t = 0
        while t < T:
            n = min(512, T - t)
            self.ttiles.append((col0 + t, n))
            t += n
        self.ktiles = []
        t = 0
        while t < T:
            n = min(128, T - t)
            self.ktiles.append((t, n))
            t += n


def build_program(stage=None, npass=2):
    nc = bass.Bass("TRN2", target_bir_lowering=False)
    P = Prog(nc)

    def din(name, shape):
        return nc.dram_tensor(name, list(shape), F32, kind="ExternalInput").ap()

    def dout(name, shape):
        return nc.dram_tensor(name, list(shape), F32, kind="ExternalOutput").ap()

    xp = din("xp", [2, SEQ, D])
    xs = din("xs", [DEC, D])
    spool = din("spool", [2, 15, 256])
    sconv = din("sconv", [2, 30, 256])
    ck = din("ck", [2, PAST, 512])
    cv = din("cv", [2, PAST, 512])
    clf = din("clf", [2, PAST, 8])
    W = {}
    for nm, shp in [("ffn1_norm", [2, D]), ("ffn1_w_gu", [2, D, 2 * DFF]), ("ffn1_w_down", [2, DFF, D]),
                    ("mix_norm", [2, D]), ("w_in", [2, D, DIN]), ("w_out", [2, D, D]),
                    ("pool_w", [2, 4, 64, 64]), ("pool_scale", [2, 256]), ("conv_w", [2, 31, 256]),
                    ("conv_b", [2, 256]), ("conv_ln_g", [2, 256]), ("conv_ln_b", [2, 256]),
                    ("q_norm", [2, 64]), ("k_norm", [2, 64]), ("forget_b", [2, 8]),
                    ("ffn2_norm", [2, D]), ("ffn2_w_gu", [2, D, 2 * DFF]), ("ffn2_w_down", [2, DFF, D])]:
        W[nm] = din(nm, shp)
    c_ident = din("c_ident", [128, 128])
    c_tri = din("c_tri", [128, 128])
    c_mask = din("c_mask", [128, 128])
    c_invcnt = din("c_invcnt", [128, 2, 16])

    y_p = dout("y_p", [2, SEQ, D])
    y_s = dout("y_s", [DEC, D])
    npool_p = dout("npool_p", [2, 2, 15, 256])
    npool_s = dout("npool_s", [2, 15, 256])
    nconv_p = dout("nconv_p", [2, 2, 30, 256])
    nconv_s = dout("nconv_s", [2, 30, 256])
    nk_p = dout("nk_p", [2, 2, SEQ, 512])
    nv_p = dout("nv_p", [2, 2, SEQ, 512])
    nf_p = dout("nf_p", [2, 2, SEQ, 8])
    nk_s = dout("nk_s", [2, DEC, 512])
    nv_s = dout("nv_s", [2, DEC, 512])
    nf_s = dout("nf_s", [2, DEC, 8])

    x = P.sb("x", [128, NCH, TTMAX], F32)
    xn = P.sb("xn", [128, NCH, TTMAX], BF16)
    cat = P.sb("cat", [128, NCH, TTMAX], BF16)
    arena_ap = P.sb("arena", [128, ARENA_BYTES // 2], BF16)
    A = Arena(arena_ap, ARENA_BYTES)
    ident32 = P.sb("ident32", [128, 128], F32)
    identb = P.sb("identb", [128, 128], BF16)
    tri32 = P.sb("tri32", [128, 128], F32)
    ones32 = P.sb("ones32", [128, 128], F32)
    onesb = P.sb("onesb", [128, 128], BF16)
    maskb = P.sb("maskb", [128, 128], BF16)
    invcnt = P.sb("invcnt", [128, 2, 16], F32)
    epsT = P.sb("epsT", [128, 1], F32)
    prm = P.sb("prm", [128, 64], F32)
    cwT = P.sb("cwT", [128, 124], F32)
    gqk = P.sb("gqk", [128, 2, 4, 64], F32)
    fbT = P.sb("fbT", [128, 2, 8], F32)
    pwb = P.sb("pwb", [128, 2, 2, 128], BF16)
    pb = [P.ps("pb%d" % i, [128, 512], F32) for i in range(8)]
    wf = P.sb("wf", [128, NCH, 8], BF16)
    lf = P.sb("lf", [128, 33, 8], F32)
    ua = P.sb("ua", [128, 33, 8], F32)
    ctok = P.sb("ctok", [128, 8, 33], F32)
    rall = P.sb("rall", [128, 33, 8], F32)
    Sp = [P.sb("Sp%d" % i, [128, 8], F32) for i in range(2)]
    alpha = P.sb("alpha", [128, 16, 8], F32)

    for t_ in (lf, ua, ctok, rall, alpha, Sp[0], Sp[1]):
        P.memset("dve", t_, 0.0)
    P.dma("sp", ident32, c_ident)
    P.dma("sp", tri32, c_tri)
    P.dma("pool", identb, c_ident)
    P.dma("pool", maskb, c_mask)
    P.dma("sp", invcnt, c_invcnt)
    P.memset("dve", ones32, 1.0)
    P.memset("dve", onesb, 1.0)
    P.memset("dve", epsT, EPS)
    A.reset()
    rows = A.alloc([64, 128], F32)
    rows2 = A.alloc([124, 128], F32)
    r = 0
    PRM = {}
    for nm, nr in [("ffn1_norm", 16), ("mix_norm", 16), ("ffn2_norm", 16), ("pool_scale", 4),
                   ("conv_b", 4), ("conv_ln_g", 4), ("conv_ln_b", 4)]:
        P.dma("sp", rows[r:r + nr, :], W[nm].rearrange("l (c p) -> (l c) p", p=128))
        PRM[nm] = r
        r += nr
    P.dma("sp", rows2, W["conv_w"].rearrange("l j (c p) -> (l j c) p", p=128))
    P.mm(pb[7][:, 0:64], rows, ident32[0:64, 0:64])
    P.copy("dve", prm, pb[7][:, 0:64])
    P.mm(pb[7][:, 128:252], rows2, ident32[0:124, 0:124])
    P.copy("dve", cwT, pb[7][:, 128:252])
    for l in range(2):
        for k in range(4):
            src = W["q_norm"] if k < 2 else W["k_norm"]
            P.dma("sp", gqk[:, l, k, :], src[l:l + 1, :].partition_broadcast(128))
        P.dma("sp", fbT[:, l, :], W["forget_b"][l:l + 1, :].partition_broadcast(128))
    P.memset("dve", pwb, 0.0)
    for l in range(2):
        for cc in range(2):
            for g2 in range(2):
                P.dma("pool", pwb[g2 * 64:(g2 + 1) * 64, l, cc, g2 * 64:(g2 + 1) * 64],
                      W["pool_w"][l, 2 * cc + g2])

    def gvec(nm, l, c):
        base = PRM[nm]
        nper = 8 if nm.endswith("norm") else 2
        k = base + l * nper + c
        return prm[:, k:k + 1]

    def load_x(seq):
        src = xp[seq.b] if seq.kind == "prompt" else xs
        for (t0, n) in seq.ktiles:
            xt = A.alloc([128, D], F32)
            P.dma("sp", xt[0:n, :], src[t0:t0 + n, :])
            for half in range(2):
                ps = pb[6 + half]
                for q in range(4):
                    dc = half * 4 + q
                    P.mm(ps[:, q * 128:q * 128 + n], xt[0:n, dc * 128:(dc + 1) * 128], ident32[0:n, 0:n])
                dst = x[:, half * 4:half * 4 + 4, seq.col0 + t0:seq.col0 + t0 + n]
                srcp = ps.rearrange("p (a b) -> p a b", a=4)[:, :, 0:n]
                P.copy("act" if half == 0 else "dve", dst, srcp)
            if A.off > ARENA_BYTES - 8192:
                A.reset()

    def store_y(seq):
        dst = y_p[seq.b] if seq.kind == "prompt" else y_s
        for (t0, n) in seq.ktiles:
            yt = A.alloc([128, D], F32)
            for half in range(2):
                ps = pb[6 + half]
                for q in range(4):
                    dc = half * 4 + q
                    P.mm(ps[0:n, q * 128:(q + 1) * 128], x[:, dc, seq.col0 + t0:seq.col0 + t0 + n], ident32)
                P.copy("act" if half == 0 else "dve", yt[0:n, half * 512:(half + 1) * 512], ps[0:n, :])
            P.dma("sp", dst[t0:t0 + n, :], yt[0:n, :])
            if A.off > ARENA_BYTES - 8192:
                A.reset()

    def norm_bufs(arena):
        sqs = [arena.alloc([128, NCH, 512], BF16) for _ in range(2)]
        sds = [arena.alloc([128, 512], F32) for _ in range(2)]
        rss = [arena.alloc([128, 512], F32) for _ in range(2)]
        return {"sq": sqs, "sd": sds, "rs": rss, "k": 0}

    def norm_tile(nm, l, c0, n, nb):
        k_ = nb["k"]
        nb["k"] += 1
        sq, sd, rs = nb["sq"][k_ % 2], nb["sd"][k_ % 2], nb["rs"][k_ % 2]
        P.act(sq[:, :, 0:n], x[:, :, c0:c0 + n], AF.Square)
        for c in range(NCH):
            P.mm(pb[6][:, 0:n], onesb, sq[:, c, 0:n], start=(c == 0), stop=(c == NCH - 1))
        P.act(sd[:, 0:n], pb[6][:, 0:n], AF.Sqrt, bias=epsT, scale=1.0 / D)
        P.op("dve", lambda e, o=rs[:, 0:n], i=sd[:, 0:n]: e.reciprocal(o, i),
             reads=[sd[:, 0:n]], writes=[rs[:, 0:n]])
        for c in range(NCH):
            P.stt("dve", xn[:, c, c0:c0 + n], x[:, c, c0:c0 + n], gvec(nm, l, c), rs[:, 0:n],
                  ALU.mult, ALU.mult)

    def rmsnorm(seqs, nm, l):
        A.reset()
        nb = norm_bufs(A)
        for seq in seqs:
            for (c0, n) in seq.ttiles:
                norm_tile(nm, l, c0, n, nb)

    def ffn(seqs, l, which, prenormed=False, next_norm=None):
        if not prenormed:
            rmsnorm(seqs, which + "_norm", l)
        wgu_d = W[which + "_w_gu"][l].rearrange("(c p) f -> p c f", p=128)
        wd_d = W[which + "_w_down"][l].rearrange("(f p) d -> p f d", p=128)
        A.reset()
        g = A.alloc([128, 6, TTMAX], BF16)
        off_wgu = A.off
        wgu = [A.alloc([128, NCH, 256], BF16) for _ in range(3)]
        wd = [A.alloc([128, 6, D], BF16) for _ in range(2)]
        st = [A.alloc([128, 512], F32) for _ in range(2)]
        tiles = [tt for s in seqs for tt in s.ttiles]

        def load_gu(fc):
            b = wgu[fc % 3]
            P.dma("pool", b[:, :, 0:128], wgu_d[:, :, fc * 128:(fc + 1) * 128])
            P.dma("pool", b[:, :, 128:256], wgu_d[:, :, DFF + fc * 128:DFF + (fc + 1) * 128])

        def load_d(gi):
            f0, nf = FGROUPS[gi]
            P.dma("pool", wd[gi % 2][:, 0:nf, :], wd_d[:, f0:f0 + nf, :])

        for fc in range(3):
            load_gu(fc)
        load_d(0)
        load_d(1)
        it = 0
        cntd = {"ky": 0}
        for gi, (f0, nf) in enumerate(FGROUPS):
            for fl in range(nf):
                fc = f0 + fl
                b = wgu[fc % 3]
                for (c0, n) in tiles:
                    pa, pu, s = pb[it % 2], pb[2 + it % 2], st[it % 2]
                    it += 1
                    for c in range(NCH):
                        P.mm(pa[:, 0:n], b[:, c, 0:128], xn[:, c, c0:c0 + n], start=(c == 0), stop=(c == NCH - 1))
                    for c in range(NCH):
                        P.mm(pu[:, 0:n], b[:, c, 128:256], xn[:, c, c0:c0 + n], start=(c == 0), stop=(c == NCH - 1))
                    P.act(s[:, 0:n], pa[:, 0:n], AF.Silu)
                    P.tt("dve", g[:, fl, c0:c0 + n], pu[:, 0:n], s[:, 0:n], ALU.mult)
                if fc + 3 < NFC:
                    load_gu(fc + 3)
            wb = wd[gi % 2]
            last = (gi == len(FGROUPS) - 1)

            def down(dc, c0, n):
                py = pb[4 + cntd["ky"] % 2]
                cntd["ky"] += 1
                for fl in range(nf):
                    P.mm(py[:, 0:n], wb[:, fl, dc * 128:(dc + 1) * 128], g[:, fl, c0:c0 + n],
                         start=(fl == 0), stop=(fl == nf - 1))
                P.stt("dve", x[:, dc, c0:c0 + n], py[:, 0:n], 0.5, x[:, dc, c0:c0 + n], ALU.mult, ALU.add)

            if last and next_norm is not None:
                A2 = Arena(arena_ap, ARENA_BYTES)
                A2.off = off_wgu
                nb = norm_bufs(A2)
                prevt = None
                for (c0, n) in tiles:
                    for dc in range(NCH):
                        down(dc, c0, n)
                    if prevt is not None:
                        norm_tile(next_norm[0], next_norm[1], prevt[0], prevt[1], nb)
                    prevt = (c0, n)
                norm_tile(next_norm[0], next_norm[1], prevt[0], prevt[1], nb)
            else:
                for dc in range(NCH):
                    for (c0, n) in tiles:
                        down(dc, c0, n)
            if gi + 2 < len(FGROUPS):
                load_d(gi + 2)

    def transpose_out(src_cols, ncols, stg, col_off, ps):
        P.mm(ps[0:ncols, 0:128], src_cols, ident32)
        P.copy("act", stg[0:ncols, col_off:col_off + 128], ps[0:ncols, 0:128])

    def pool_part(seq, l):
        T = seq.T
        win_d = W["w_in"][l].rearrange("(c p) f -> p c f", p=128)
        A.reset()
        wp = [A.alloc([128, NCH, 128], BF16) for _ in range(2)]
        ups = [A.alloc([128, 16 + SEQ], F32) for _ in range(2)]
        sa = A.alloc([128, 16 + SEQ], F32)
        sbf = A.alloc([128, 16 + SEQ], F32)
        dds = [A.alloc([128, SEQ], BF16) for _ in range(2)]
        stg = A.alloc([16, 256], F32)
        tmp16 = A.alloc([128, 16], F32)
        for cc in range(2):
            P.dma("pool", wp[cc], win_d[:, :, cc * 128:(cc + 1) * 128])
        if seq.kind == "sample":
            prev = A.alloc([16, 256], F32)
            P.dma("sp", prev[0:15, :], spool[l])
        kk = 0
        for cc in range(2):
            up = ups[cc]
            if seq.kind == "prompt":
                P.memset("dve", up[:, 0:16], 0.0)
            else:
                P.memset("dve", up[:, 0:1], 0.0)
                P.mm(pb[7][:, 0:15], prev[0:15, cc * 128:(cc + 1) * 128], ident32[0:15, 0:15])
                P.copy("act", up[:, 1:16], pb[7][:, 0:15])
            for (c0, n) in seq.ttiles:
                ps = pb[kk % 2]
                kk += 1
                for c in range(NCH):
                    P.mm(ps[:, 0:n], wp[cc][:, c, :], xn[:, c, c0:c0 + n], start=(c == 0), stop=(c == NCH - 1))
                t0 = c0 - seq.col0
                P.copy("act", up[:, 16 + t0:16 + t0 + n], ps[:, 0:n])
            transpose_out(up[:, T + 1:T + 16], 15, stg, cc * 128, pb[7][:, 256:384])
        dst = npool_p[l, seq.b] if seq.kind == "prompt" else npool_s[l]
        P.dma("sp", dst, stg[0:15, :])
        for cc in range(2):
            up, dd = ups[cc], dds[cc]
            L = 16 + T
            P.tt("dve", sa[:, 2:L], up[:, 2:L], up[:, 1:L - 1], ALU.add)
            P.tt("dve", sbf[:, 4:L], sa[:, 4:L], sa[:, 2:L - 2], ALU.add)
            if cc == 0:
                lo, hi, wl, wh = sa, sbf, 2.0, 4.0
            else:
                P.tt("dve", sa[:, 8:L], sbf[:, 8:L], sbf[:, 4:L - 4], ALU.add)
                P.tt("dve", sbf[:, 16:L], sa[:, 16:L], sa[:, 8:L - 8], ALU.add)
                lo, hi, wl, wh = sa, sbf, 8.0, 16.0
            for (pr, srcb, w) in ((slice(0, 64), lo, wl), (slice(64, 128), hi, wh)):
                P.stt("dve", dd[pr, 0:T], srcb[pr, 16:16 + T], 1.0 / w, up[pr, 16:16 + T], ALU.mult, ALU.subtract)
                if seq.kind == "prompt":
                    P.tt("dve", tmp16[pr, :], srcb[pr, 16:32], invcnt[pr, cc, :], ALU.mult)
                    P.tt("dve", dd[pr, 0:16], tmp16[pr, :], up[pr, 16:32], ALU.subtract)
            for (c0, n) in seq.ttiles:
                ps = pb[2 + kk % 2]
                kk += 1
                t0 = c0 - seq.col0
                P.mm(ps[:, 0:n], pwb[:, l, cc, :], dd[:, t0:t0 + n])
                P.act(cat[:, cc, c0:c0 + n], ps[:, 0:n], AF.Copy, scale=gvec("pool_scale", l, cc))

    def conv_part(seq, l, hook=None):
        T = seq.T
        win_d = W["w_in"][l].rearrange("(c p) f -> p c f", p=128)
        A.reset()
        wa = [A.alloc([128, NCH, 128], BF16) for _ in range(2)]
        wg = [A.alloc([128, NCH, 128], BF16) for _ in range(2)]
        z = A.alloc([128, 2, 32 + SEQ], F32)
        acc = A.alloc([128, 2, SEQ], F32)
        sg = [A.alloc([128, 512], F32) for _ in range(2)]
        sq = A.alloc([128, 2, 512], F32)
        mu = A.alloc([128, 512], F32)
        m2 = A.alloc([128, 512], F32)
        sd = A.alloc([128, 512], F32)
        rs = A.alloc([128, 512], F32)
        t1 = [A.alloc([128, 512], F32) for _ in range(2)]
        stg = A.alloc([32, 256], F32)
        for cc in range(2):
            P.dma("pool", wa[cc], win_d[:, :, 256 + cc * 128:256 + (cc + 1) * 128])
            P.dma("pool", wg[cc], win_d[:, :, 512 + cc * 128:512 + (cc + 1) * 128])
        if seq.kind == "sample":
            prev = A.alloc([32, 256], F32)
            P.dma("sp", prev[0:30, :], sconv[l])
        kk = 0
        for cc in range(2):
            if seq.kind == "prompt":
                P.memset("dve", z[:, cc, 0:32], 0.0)
            else:
                P.memset("dve", z[:, cc, 0:2], 0.0)
                P.mm(pb[7][:, 0:30], prev[0:30, cc * 128:(cc + 1) * 128], ident32[0:30, 0:30])
                P.copy("act", z[:, cc, 2:32], pb[7][:, 0:30])
            for (c0, n) in seq.ttiles:
                pa, pg, s = pb[kk % 2], pb[2 + kk % 2], sg[kk % 2]
                kk += 1
                for c in range(NCH):
                    P.mm(pa[:, 0:n], wa[cc][:, c, :], xn[:, c, c0:c0 + n], start=(c == 0), stop=(c == NCH - 1))
                for c in range(NCH):
                    P.mm(pg[:, 0:n], wg[cc][:, c, :], xn[:, c, c0:c0 + n], start=(c == 0), stop=(c == NCH - 1))
                t0 = c0 - seq.col0
                P.act(s[:, 0:n], pg[:, 0:n], AF.Sigmoid)
                P.tt("dve", z[:, cc, 32 + t0:32 + t0 + n], pa[:, 0:n], s[:, 0:n], ALU.mult)
            transpose_out(z[:, cc, T + 2:T + 32], 30, stg, cc * 128, pb[7][:, 256:384])
        def tap(j, cc):
            k = (l * 31 + j) * 2 + cc
            return cwT[:, k:k + 1]
        for cc in range(2):
            P.ts("dve", acc[:, cc, 0:T], z[:, cc, 2:2 + T], tap(0, cc), gvec("conv_b", l, cc), ALU.mult, ALU.add)
        for j in range(1, 31):
            for cc in range(2):
                P.stt("dve", acc[:, cc, 0:T], z[:, cc, 2 + j:2 + j + T], tap(j, cc), acc[:, cc, 0:T],
                      ALU.mult, ALU.add)
                if hook is not None:
                    hook()
        dst = nconv_p[l, seq.b] if seq.kind == "prompt" else nconv_s[l]
        P.dma("sp", dst, stg[0:30, :])
        kk = 0
        for (c0, n) in seq.ttiles:
            t0 = c0 - seq.col0
            a2 = acc[:, :, t0:t0 + n]
            P.act(sq[:, :, 0:n], a2, AF.Square)
            for cc in range(2):
                P.mm(pb[4][:, 0:n], ones32, acc[:, cc, t0:t0 + n], start=(cc == 0), stop=(cc == 1))
            for cc in range(2):
                P.mm(pb[5][:, 0:n], ones32, sq[:, cc, 0:n], start=(cc == 0), stop=(cc == 1))
            P.act(mu[:, 0:n], pb[4][:, 0:n], AF.Copy, scale=1.0 / 256)
            P.tt("dve", m2[:, 0:n], mu[:, 0:n], mu[:, 0:n], ALU.mult)
            P.stt("dve", m2[:, 0:n], pb[5][:, 0:n], 1.0 / 256, m2[:, 0:n], ALU.mult, ALU.subtract)
            P.ts("dve", m2[:, 0:n], m2[:, 0:n], 0.0, None, ALU.max)
            P.act(sd[:, 0:n], m2[:, 0:n], AF.Sqrt, bias=epsT, scale=1.0)
            P.op("dve", lambda e, o=rs[:, 0:n], i=sd[:, 0:n]: e.reciprocal(o, i),
                 reads=[sd[:, 0:n]], writes=[rs[:, 0:n]])
            for cc in range(2):
                t = t1[kk % 2]
                kk += 1
                P.tt("dve", t[:, 0:n], acc[:, cc, t0:t0 + n], mu[:, 0:n], ALU.subtract)
                P.tt("dve", t[:, 0:n], t[:, 0:n], rs[:, 0:n], ALU.mult)
                P.act(cat[:, 2 + cc, c0:c0 + n], t[:, 0:n], AF.Silu,
                      bias=gvec("conv_ln_b", l, cc), scale=gvec("conv_ln_g", l, cc))

    def att_prologue(seq, l):
        T = seq.T
        sample = seq.kind == "sample"
        win_d = W["w_in"][l].rearrange("(c p) f -> p c f", p=128)
        NN = len(seq.ktiles)
        NP = PAST // 128 if sample else 0
        NT = NP + NN
        P.dma("pool", wf, win_d[:, :, 2304:2312])
        if sample:
            cl = clf[l].rearrange("(j p) h -> p j h", p=128)
            for q in range(4):
                P.dma("sp", lf[:, q * 8:(q + 1) * 8, :], cl[:, q * 8:(q + 1) * 8, :])
        for ti, (t0, n) in enumerate(seq.ktiles):
            ps = pb[6][0:n, ti * 8:(ti + 1) * 8]
            cs = seq.col0 + t0
            for c in range(NCH):
                P.mm(ps, xn[:, c, cs:cs + n], wf[:, c, :], start=(c == 0), stop=(c == NCH - 1))
            P.tt("dve", ua[0:n, ti, :], ps, fbT[0:n, l, :], ALU.add)
            yield
        nrow = seq.ktiles[0][1]
        uv = ua[0:nrow, 0:NN, :]
        P.act(uv, uv, AF.Exp, scale=-1.0)
        P.act(uv, uv, AF.Ln, bias=1.0)
        P.ts("dve", lf[0:nrow, NP:NP + NN, :], uv, -1.0, None, ALU.mult)
        if sample:
            P.dma("sp", nf_s[l], lf[0:nrow, NP, :])
        else:
            P.dma("sp", nf_p[l, seq.b].rearrange("(j p) h -> p j h", p=128), lf[:, 0:NN, :])
        yield
        P.memset("dve", Sp[0], 0.0)
        P.memset("dve", rall[:, 0, :], 0.0)
        for j in range(NT):
            n = 128 if j < NP else seq.ktiles[j - NP][1]
            ps = pb[6][0:n, 256 + (j % 16) * 8:256 + (j % 16) * 8 + 8]
            s_cur, s_nxt = Sp[j % 2], Sp[(j + 1) % 2]
            P.mm(ps, tri32[0:n, 0:n], lf[0:n, j, :], start=True, stop=(j == 0))
            if j > 0:
                P.mm(ps, ones32[:, 0:n], s_cur, start=False, stop=True)
                pr = pb[6][:, 384 + (j % 16) * 8:384 + (j % 16) * 8 + 8]
                P.mm(pr, ones32, s_cur)
                P.copy("act", rall[:, j, :], pr)
            P.copy("act", ctok[0:n, :, j], ps)
            if j + 1 < NT:
                P.tt("dve", s_nxt, s_cur, lf[:, j, :], ALU.add)
            yield
        if not sample:
            for I in range(4):
                P.tt("dve", alpha[:, 4 * I:4 * I + 4, :], rall[:, 4 * I:4 * I + 4, :],
                     rall[:, 4 * I:4 * I + 1, :].broadcast_to([128, 4, 8]), ALU.subtract)
            P.act(alpha, alpha, AF.Exp)
        yield

    def att_part(seq, l):
        T = seq.T
        sample = seq.kind == "sample"
        win_d = W["w_in"][l].rearrange("(c p) f -> p c f", p=128)
        NN = len(seq.ktiles)
        NP = PAST // 128 if sample else 0
        NT = NP + NN
        A.reset()
        biasb = [A.alloc([128, 2, 33], F32) for _ in range(2)]
        wqkv = [A.alloc([128, NCH, 384], BF16) for _ in range(2)]
        QT = A.alloc([128, SEQ], BF16)
        KT = A.alloc([128, SEQ], BF16)
        VA = A.alloc([128, 16, 2, 128], BF16)
        sqb = [A.alloc([128, 256], F32) for _ in range(2)]
        ssb = [A.alloc([128, 4], F32) for _ in range(2)]
        sdb = [A.alloc([128, 4], F32) for _ in range(2)]
        qkn = [A.alloc([128, 4, 64], F32) for _ in range(3)]
        vst = [A.alloc([128, 128], F32) for _ in range(3)]
        PTW = 128 if sample else 512
        PT = [A.alloc([128, 2, PTW], BF16) for _ in range(3)]
        rec = [A.alloc([128, 2, 128], F32) for _ in range(2)]
        if not sample:
            toff = [A.alloc([128, 2, 128], F32) for _ in range(2)]
        if sample:
            kraw = [A.alloc([128, 4, 128], F32) for _ in range(3)]
            vraw32 = [A.alloc([128, 4, 128], F32) for _ in range(3)]
            KTg = [A.alloc([128, 4, 128], BF16) for _ in range(2)]
            ebb = [A.alloc([128, 2, 33], F32) for _ in range(2)]
            vraw = [A.alloc([128, 4, 2, 128], BF16) for _ in range(3)]
            for v in vraw:
                P.memset("dve", v[:, :, :, 64:128], 1.0)
        P.memset("dve", VA[:, :, :, 64:128], 1.0)

        def load_qkv(p):
            b = wqkv[p % 2]
            for k in range(3):
                P.dma("pool", b[:, :, k * 128:(k + 1) * 128],
                      win_d[:, :, 768 + 512 * k + p * 128:768 + 512 * k + (p + 1) * 128])

        load_qkv(0)
        load_qkv(1)
        if ATT_LEVEL < 3:
            return
        qtiles = [(NP + i, t0, n) for i, (t0, n) in enumerate(seq.ktiles)]
        cnt = {"s": 0, "o": 0, "pt": 0, "kr": 0, "b": 0, "q": 0, "r": 0, "tp": 0}
        dstk = nk_s[l] if sample else nk_p[l, seq.b]
        dstv = nv_s[l] if sample else nv_p[l, seq.b]
        QKB = [pb[0], pb[2], pb[3]]
        TRB = [pb[1], pb[4]]
        for p in range(4):
            wb = wqkv[p % 2]
            def projA(ti, t0, n):
                cs = seq.col0 + t0
                q_ = cnt["q"]
                cnt["q"] += 1
                ps = QKB[q_ % 3]
                sq_, ss_, sd_ = sqb[q_ % 2], ssb[q_ % 2], sdb[q_ % 2]
                qk, vs = qkn[q_ % 3], vst[q_ % 3]
                for c in range(NCH):
                    P.mm(ps[0:n, 0:384], xn[:, c, cs:cs + n], wb[:, c, :], start=(c == 0), stop=(c == NCH - 1))
                P.act(sq_[0:n, :], ps[0:n, 0:256], AF.Square)
                P.op("dve", lambda e, o=ss_[0:n, :], i=sq_[0:n, :].rearrange("p (a b) -> p a b", a=4):
                     e.tensor_reduce(o, i, AX.X, ALU.add), reads=[sq_[0:n, :]], writes=[ss_[0:n, :]])
                P.act(sd_[0:n, :], ss_[0:n, :], AF.Sqrt, bias=epsT[0:n, :], scale=1.0 / 64)
                P.op("dve", lambda e, o=ss_[0:n, :], i=sd_[0:n, :]: e.reciprocal(o, i),
                     reads=[sd_[0:n, :]], writes=[ss_[0:n, :]])
                for k4 in range(4):
                    P.stt("dve", qk[0:n, k4, :], ps[0:n, k4 * 64:(k4 + 1) * 64], ss_[0:n, k4:k4 + 1],
                          gqk[0:n, l, k4, :], ALU.mult, ALU.mult)
                P.copy("act", vs[0:n, :], ps[0:n, 256:384])
                P.copy("dve", VA[0:n, ti, :, 0:64], ps[0:n, 256:384].rearrange("p (a b) -> p a b", a=2))
                P.dma("sp", dstk[t0:t0 + n, p * 128:(p + 1) * 128], qk[0:n, 2:4, :].rearrange("p a b -> p (a b)"))
                P.dma("sp", dstv[t0:t0 + n, p * 128:(p + 1) * 128], vs[0:n, :])
                return (qk, t0, n)

            def projB(st):
                qk, t0, n = st
                pt_ = TRB[cnt["tp"] % 2]
                cnt["tp"] += 1
                P.mm(pt_[:, 0:n], qk[0:n, 0:2, :].rearrange("p a b -> p (a b)"), ident32[0:n, 0:n])
                P.mm(pt_[:, 128:128 + n], qk[0:n, 2:4, :].rearrange("p a b -> p (a b)"), ident32[0:n, 0:n])
                P.copy("act", QT[:, t0:t0 + n], pt_[:, 0:n])
                P.copy("act", KT[:, t0:t0 + n], pt_[:, 128:128 + n])

            pend = []
            for ti, (t0, n) in enumerate(seq.ktiles):
                pend.append(projA(ti, t0, n))
                if len(pend) > 2:
                    projB(pend.pop(0))
            while pend:
                projB(pend.pop(0))
            if p + 2 < 4:
                load_qkv(p + 2)
            if ATT_LEVEL < 4:
                continue
            items = []
            if sample:
                for (gi, t0, nq) in qtiles:
                    for j in range(gi + 1):
                        items.append(("diag", gi, t0, nq, j, 0, gi))
            else:
                for I in range(4):
                    for j in range(4 * I):
                        items.append(("off", 4 * I, 512 * I, 512, j, 0, 4 * I - 1))
                    for i in range(4 * I, 4 * I + 4):
                        for j in range(4 * I, i + 1):
                            items.append(("diag", i, 128 * i, 128, j, 4 * I, i))
            state = {"kgrp": None, "bgi": None}
            groups = {}

            def prep_dma(g):
                kb, v32 = kraw[g % 3], vraw32[g % 3]
                j0 = g * 4
                P.dma("sp", kb, ck[l, j0 * 128:(j0 + 4) * 128, p * 128:(p + 1) * 128]
                      .rearrange("(j p) c -> p j c", p=128))
                P.dma("sp", v32, cv[l, j0 * 128:(j0 + 4) * 128, p * 128:(p + 1) * 128]
                      .rearrange("(j p) c -> p j c", p=128))

            def prep_tr(g):
                kb, v32, vb, ktg = kraw[g % 3], vraw32[g % 3], vraw[g % 3], KTg[g % 2]
                for q4 in range(4):
                    P.mm(pb[1][:, q4 * 128:(q4 + 1) * 128], kb[:, q4, :], ident32)
                P.copy("dve", ktg, pb[1].rearrange("p (a b) -> p a b", a=4))
                eb = state["eb"]
                for q4 in range(4):
                    for hh in range(2):
                        sc = eb[:, hh, 4 * g + q4:4 * g + q4 + 1]
                        P.act(vb[:, q4, hh, 0:64], v32[:, q4, hh * 64:(hh + 1) * 64], AF.Copy, scale=sc)
                        P.ts("dve", vb[:, q4, hh, 64:128], ones32[:, 0:64], sc, None, ALU.mult)
                groups[g] = (ktg, vb)

            def scores(it):
                kind, gi, t0, nq, j, jf, jl_ = it
                if state["bgi"] != gi:
                    bb = biasb[cnt["b"] % 2]
                    cnt["b"] += 1
                    for hh in range(2):
                        P.ts("dve", bb[:, hh, 0:gi + 1], ctok[:, 2 * p + hh, 0:gi + 1], -1.0,
                             rall[:, gi, 2 * p + hh:2 * p + hh + 1], ALU.mult, ALU.add)
                    state["bb"] = bb
                    state["bgi"] = gi
                if j == jf:
                    if kind == "off":
                        state["po"] = None
                    else:
                        state["po"] = pb[6 + cnt["o"] % 2].rearrange("p (a b) -> p a b", a=4)
                        cnt["o"] += 1
                bb, po = state["bb"], state["po"]
                psS = [pb[2 + cnt["s"] % 2], pb[4 + cnt["s"] % 2]]
                cnt["s"] += 1
                ptile = PT[cnt["pt"] % 3]
                cnt["pt"] += 1
                diag = (kind == "diag" and j == gi)
                if j < NP:
                    g = j // 4
                    if j % 4 == 0:
                        ng = NP // 4
                        if g == 0:
                            prep_dma(0)
                            prep_dma(1)
                            prep_tr(0)
                        if g + 2 < ng:
                            prep_dma(g + 2)
                        if g + 1 < ng:
                            prep_tr(g + 1)
                    ktg, vb = groups[g]
                    nk = 128
                    kT = [ktg[hh * 64:(hh + 1) * 64, j % 4, :] for hh in range(2)]
                    vv = [vb[:, j % 4, hh, :] for hh in range(2)]
                else:
                    jj = j - NP
                    k0, nk = seq.ktiles[jj]
                    kT = [KT[hh * 64:(hh + 1) * 64, k0:k0 + nk] for hh in range(2)]
                    vv = [VA[0:nk, jj, hh, :] for hh in range(2)]
                for hh in range(2):
                    P.mm(psS[hh][0:nk, 0:nq], kT[hh], QT[hh * 64:(hh + 1) * 64, t0:t0 + nq],
                         start=True, stop=not diag)
                if diag:
                    for hh in range(2):
                        P.mm(psS[hh][0:nk, 0:nq], identb[:, 0:nk], maskb[:, 0:nq], start=False, stop=True)
                return dict(it=it, psS=psS, ptile=ptile, nk=nk, vv=vv, bb=bb, po=po)

            def exps(d):
                kind, gi, t0, nq, j, jf, jl_ = d["it"]
                nk = d["nk"]
                for hh in range(2):
                    P.act(d["ptile"][0:nk, hh, 0:nq], d["psS"][hh][0:nk, 0:nq], AF.Exp,
                          bias=d["bb"][0:nk, hh, j:j + 1], scale=0.125)

            def pv(d):
                kind, gi, t0, nq, j, jf, jl_ = d["it"]
                nk, po = d["nk"], d["po"]
                if kind == "off":
                    for hh in range(2):
                        P.mm(pb[hh][:, 0:nq], d["vv"][hh], d["ptile"][0:nk, hh, 0:nq], start=(j == jf), stop=(j == jl_))
                    return
                for hh in range(2):
                    P.mm(po[:, hh, 0:nq], d["vv"][hh], d["ptile"][0:nk, hh, 0:nq], start=(j == jf and hh == 0),
                         stop=(j == jl_), skip=True)
                if j == jl_:
                    rc = rec[cnt["r"] % 2]
                    cnt["r"] += 1
                    src = po
                    if (not sample) and gi >= 4:
                        tf = toff[cnt["r"] % 2]
                        sub = (gi % 4) * 128
                        for hh in range(2):
                            P.ts("dve", tf[:, hh, :], pb[hh][:, sub:sub + 128],
                                 alpha[:, gi, 2 * p + hh:2 * p + hh + 1], None, ALU.mult)
                        P.tt("dve", tf[:, :, 0:nq], tf[:, :, 0:nq], po[:, 0:2, 0:nq], ALU.add)
                        src = tf
                    P.op("dve", lambda e, o=rc[64:128, :, 0:nq], i=src[64:128, 0:2, 0:nq]: e.reciprocal(o, i),
                         reads=[src[64:128, 0:2, 0:nq]], writes=[rc[64:128, :, 0:nq]])
                    P.copy("dve", rc[0:64, :, 0:nq], rc[64:128, :, 0:nq])
                    cs = seq.col0 + t0
                    P.tt("dve", cat[0:64, 4 + p, cs:cs + nq], src[0:64, 0, 0:nq], rc[0:64, 0, 0:nq], ALU.mult)
                    P.tt("dve", rc[0:64, 0, 0:nq], src[0:64, 1, 0:nq], rc[0:64, 1, 0:nq], ALU.mult)
                    P.copy("dve", cat[64:128, 4 + p, cs:cs + nq], rc[0:64, 0, 0:nq])

            if sample:
                gi, t0, nq = qtiles[0]
                bb = biasb[cnt["b"] % 2]
                cnt["b"] += 1
                ebt = ebb[p % 2]
                for hh in range(2):
                    P.ts("dve", bb[:, hh, 0:gi + 1], ctok[:, 2 * p + hh, 0:gi + 1], -1.0,
                         rall[:, gi, 2 * p + hh:2 * p + hh + 1], ALU.mult, ALU.add)
                P.act(ebt[:, :, 0:NP], bb[:, :, 0:NP], AF.Exp)
                state["eb"] = ebt
                state["bb"] = bb
                state["bgi"] = gi
                po = pb[6 + cnt["o"] % 2].rearrange("p (a b) -> p a b", a=4)
                cnt["o"] += 1
                state["po"] = po
                ng = NP // 4
                prep_dma(0)
                prep_dma(1)
                prep_tr(0)

                def g_scores(g):
                    if g + 2 < ng:
                        prep_dma(g + 2)
                    if g + 1 < ng:
                        prep_tr(g + 1)
                    ktg, vb = groups[g]
                    psS = [pb[2 + cnt["s"] % 2], pb[4 + cnt["s"] % 2]]
                    cnt["s"] += 1
                    ptile = PT[cnt["pt"] % 3]
                    cnt["pt"] += 1
                    for q4 in range(4):
                        for hh in range(2):
                            P.mm(psS[hh][:, q4 * 32:q4 * 32 + nq], ktg[hh * 64:(hh + 1) * 64, q4, :],
                                 QT[hh * 64:(hh + 1) * 64, t0:t0 + nq], start=True, stop=True)
                    return (g, psS, ptile, vb)

                def g_exps(d):
                    g, psS, ptile, vb = d
                    for hh in range(2):
                        P.act(ptile[:, hh, 0:128], psS[hh][:, 0:128], AF.Exp, scale=0.125)

                def g_pv(d):
                    g, psS, ptile, vb = d
                    for q4 in range(4):
                        for hh in range(2):
                            P.mm(po[:, hh, 0:nq], vb[:, q4, hh, :], ptile[:, hh, q4 * 32:q4 * 32 + nq],
                                 start=(g == 0 and q4 == 0 and hh == 0), stop=False, skip=True)

                prevg = None
                for g in range(ng):
                    d = g_scores(g)
                    if prevg is not None:
                        g_pv(prevg)
                    g_exps(d)
                    prevg = d
                it = ("diag", gi, t0, nq, gi, 0, gi)
                dd = scores(it)
                g_pv(prevg)
                exps(dd)
                pv(dd)
            else:
                prevd = None
                for it in items:
                    d = scores(it)
                    if prevd is not None:
                        pv(prevd)
                    exps(d)
                    prevd = d
                pv(prevd)

    def wout_part(seqs, l, next_norm=None):
        A.reset()
        wo = A.alloc([128, NCH, D], BF16)
        P.dma("pool", wo, W["w_out"][l].rearrange("(c p) d -> p c d", p=128))
        nb = norm_bufs(A) if next_norm is not None else None
        tiles = [tt for s in seqs for tt in s.ttiles]
        k = 0
        prevt = None
        for (c0, n) in tiles:
            for dc in range(NCH):
                ps = pb[k % 2]
                k += 1
                for kc in range(NCH):
                    P.mm(ps[:, 0:n], wo[:, kc, dc * 128:(dc + 1) * 128], cat[:, kc, c0:c0 + n],
                         start=(kc == 0), stop=(kc == NCH - 1))
                P.tt("dve", x[:, dc, c0:c0 + n], ps[:, 0:n], x[:, dc, c0:c0 + n], ALU.add)
            if next_norm is not None:
                if prevt is not None:
                    norm_tile(next_norm[0], next_norm[1], prevt[0], prevt[1], nb)
                prevt = (c0, n)
        if next_norm is not None:
            norm_tile(next_norm[0], next_norm[1], prevt[0], prevt[1], nb)

    ORDER = ["load", "ffn1", "norm", "pool", "conv", "att", "wout", "ffn2"]
    def upto(name, l):
        if stage is None:
            return True
        sl, sn = stage
        if l < sl:
            return True
        if l > sl:
            return False
        return ORDER.index(name) <= ORDER.index(sn)
    passes = [[Seq("prompt", 0, SEQ, 0)], [Seq("prompt", 0, SEQ, 1), Seq("sample", SEQ, DEC, 0)]][:npass]
    for seqs in passes:
        A.reset()
        for s in seqs:
            load_x(s)
        for l in range(2):
            full = stage is None
            if upto("ffn1", l):
                ffn(seqs, l, "ffn1", prenormed=(full and l > 0), next_norm=("mix_norm", l) if full else None)
            if upto("norm", l) and not full:
                rmsnorm(seqs, "mix_norm", l)
            for s in seqs:
                gen = att_prologue(s, l) if upto("att", l) else iter(())
                if upto("pool", l):
                    pool_part(s, l)
                if upto("conv", l):
                    conv_part(s, l, hook=lambda g=gen: next(g, None))
                for _ in gen:
                    pass
                if upto("att", l):
                    att_part(s, l)
            if upto("wout", l):
                wout_part(seqs, l, next_norm=("ffn2_norm", l) if full else None)
            if upto("ffn2", l):
                ffn(seqs, l, "ffn2", prenormed=full,
                    next_norm=("ffn1_norm", l + 1) if (full and l + 1 < 2) else None)
        A.reset()
        for s in seqs:
            store_y(s)
    P.emit()
    return nc


_NC_CACHE = {}


def _consts():
    ident = np.eye(128, dtype=np.float32)
    tri = np.triu(np.ones((128, 128), dtype=np.float32))
    kk = np.arange(128)[:, None]
    qq = np.arange(128)[None, :]
    mask = np.where(kk > qq, -30000.0, 0.0).astype(np.float32)
    inv = np.zeros((128, 2, 16), dtype=np.float32)
    wins = (2, 4, 8, 16)
    for cc in range(2):
        for p in range(128):
            w = wins[2 * cc + p // 64]
            for t in range(16):
                inv[p, cc, t] = 1.0 / min(t + 1, w)
    return ident, tri, mask, inv


def kernel(**inputs):
    if "nc" not in _NC_CACHE:
        _NC_CACHE["nc"] = build_program()
    nc = _NC_CACHE["nc"]
    f = lambda a: np.ascontiguousarray(np.asarray(a, dtype=np.float32))
    ident, tri, mask, inv = _consts()
    wnames = ["ffn1_norm", "ffn1_w_gu", "ffn1_w_down", "mix_norm", "w_in", "w_out", "pool_w", "pool_scale",
              "conv_w", "conv_b", "conv_ln_g", "conv_ln_b", "q_norm", "k_norm", "forget_b",
              "ffn2_norm", "ffn2_w_gu", "ffn2_w_down"]
    shared = {k: f(inputs[k]) for k in wnames}
    shared.update(c_ident=ident, c_tri=tri, c_mask=mask, c_invcnt=inv)
    xpr, xsm = f(inputs["x_prompt"]), f(inputs["x_sample"])
    sp_, sc_ = f(inputs["state_pool"]), f(inputs["state_conv"])
    ck_, cv_, cl_ = f(inputs["cache_k"]), f(inputs["cache_v"]), f(inputs["cache_logf"])
    in_maps = []
    for c in range(8):
        m = dict(shared)
        m["xp"] = np.ascontiguousarray(xpr[2 * c:2 * c + 2])
        m["xs"] = np.ascontiguousarray(xsm[c])
        m["spool"] = np.ascontiguousarray(sp_[:, c])
        m["sconv"] = np.ascontiguousarray(sc_[:, c])
        m["ck"] = np.ascontiguousarray(ck_[:, c].reshape(2, PAST, 512))
        m["cv"] = np.ascontiguousarray(cv_[:, c].reshape(2, PAST, 512))
        m["clf"] = np.ascontiguousarray(cl_[:, c])
        in_maps.append(m)
    res = run_bass_kernel_spmd(nc, in_maps, core_ids=list(range(8)))
    R = res.results
    cat0 = lambda k: np.concatenate([r[k] for r in R], axis=0)
    cat1 = lambda k: np.concatenate([r[k] for r in R], axis=1)
    st1 = lambda k: np.stack([r[k] for r in R], axis=1)
    y_prompt = cat0("y_p")
    y_sample = np.stack([r["y_s"] for r in R], axis=0)
    outs = (
        y_prompt, y_sample,
        cat1("npool_p"), st1("npool_s"),
        cat1("nconv_p"), st1("nconv_s"),
        cat1("nk_p").reshape(2, 16, SEQ, 8, 64), cat1("nv_p").reshape(2, 16, SEQ, 8, 64), cat1("nf_p"),
        st1("nk_s").reshape(2, 8, DEC, 8, 64), st1("nv_s").reshape(2, 8, DEC, 8, 64), st1("nf_s"),
    )
    return tuple(np.ascontiguousarray(o, dtype=np.float32) for o in outs)
```
